# Optimizing a Trainium2 kernel written in Bass

```python
import math
import jax, jax.numpy as jnp
from jax import lax
import numpy as np

D_MODEL = 1024
BATCH = 4
SEQ = 8192
DEPTH = 1

CTX_LEN = 256
GRID_W = 64
CHUNK = 64
EPS = 1e-6
A_HEADS = 8
A_DK = 128
A_DV = 128
A_CONV = 3
B_HEADS = 8
B_DK = 128
B_DV = 128
P_HEADS = 8
P_NKEYS = 128
P_EXPERTS = P_NKEYS * P_NKEYS
P_TOPK = 16
P_DQ = 128
P_BLOCK = 128

A_QKV = A_HEADS * (2 * A_DK + A_DV)
A_GATE = A_HEADS * A_DV
A_AB = 4 * A_HEADS
B_QK = B_HEADS * B_DK
B_V = B_HEADS * B_DV
MERGE_G = 2 * D_MODEL
SPLIT_IDX = (A_QKV,
             A_QKV + A_GATE,
             A_QKV + A_GATE + A_AB,
             A_QKV + A_GATE + A_AB + B_QK,
             A_QKV + A_GATE + A_AB + 3 * B_QK,
             A_QKV + A_GATE + A_AB + 3 * B_QK + B_V,
             A_QKV + A_GATE + A_AB + 3 * B_QK + 2 * B_V)
N_IN = A_QKV + A_GATE + A_AB + 3 * B_QK + 2 * B_V + MERGE_G

kernel_name = 'hybrid_gdn_hgrn2_peer_prefix_dit'


def _rmsnorm(x, w):
    xf = x.astype(jnp.float32)
    y = xf * lax.rsqrt(jnp.mean(xf * xf, axis=-1, keepdims=True) + EPS)
    return (y * w.astype(jnp.float32)).astype(x.dtype)


def _l2norm(x):
    return x * lax.rsqrt(jnp.sum(x * x, axis=-1, keepdims=True) + EPS)


def _modulate(xn, shift, scale):
    return xn * (1 + scale) + shift


def _heads(x, n_heads):
    b, l, _ = x.shape
    return x.reshape(b, l, n_heads, -1).transpose(0, 2, 1, 3)


def _unheads(x):
    b, h, l, d = x.shape
    return x.transpose(0, 2, 1, 3).reshape(b, l, h * d)


def _flip(t):
    return jnp.flip(t, axis=2)


def _short_conv(x, w):
    pad = w.shape[0] // 2
    return lax.conv_general_dilated(x, w[:, None, :].astype(x.dtype), window_strides=(1,),
                                    padding=[(pad, pad)], dimension_numbers=('NWC', 'WIO', 'NWC'),
                                    feature_group_count=x.shape[-1])


def _gdn_chunked(q, k, v, g, beta, s0):
    b, h, L, _ = q.shape
    dv = v.shape[-1]
    n = L // CHUNK
    q, k, v = (t.reshape(b, h, n, CHUNK, -1) for t in (q, k, v))
    g, beta = (t.reshape(b, h, n, CHUNK) for t in (g, beta))
    gc = jnp.cumsum(g, axis=-1)
    incl = jnp.tril(jnp.ones((CHUNK, CHUNK), bool))
    strict = jnp.tril(jnp.ones((CHUNK, CHUNK), bool), -1)
    diff = gc[..., :, None] - gc[..., None, :]
    decay = jnp.where(incl, jnp.exp(jnp.where(incl, diff, 0.0)), 0.0)
    kb = k * beta[..., None]
    m = jnp.where(strict, jnp.einsum('bhnid,bhnjd->bhnij', kb, k) * decay, 0.0)
    a = m + jnp.eye(CHUNK, dtype=m.dtype)
    rhs = jnp.concatenate([v * beta[..., None], kb * jnp.exp(gc)[..., None]], axis=-1)
    sol = lax.linalg.triangular_solve(a, rhs, left_side=True, lower=True, unit_diagonal=True)
    u, w = sol[..., :dv], sol[..., dv:]
    qk = jnp.einsum('bhnid,bhnjd->bhnij', q, k) * decay
    q_dec = q * jnp.exp(gc)[..., None]
    k_dec = k * jnp.exp(gc[..., -1:] - gc)[..., None]
    tot = jnp.exp(gc[..., -1])

    def step(S, xs):
        u_c, w_c, qd_c, kd_c, qk_c, tot_c = xs
        v_new = u_c - jnp.einsum('bhcd,bhdv->bhcv', w_c, S)
        o_c = jnp.einsum('bhcd,bhdv->bhcv', qd_c, S) + jnp.einsum('bhij,bhjv->bhiv', qk_c, v_new)
        S = tot_c[..., None, None] * S + jnp.einsum('bhcd,bhcv->bhdv', kd_c, v_new)
        return S, o_c

    xs = tuple(jnp.moveaxis(t, 2, 0) for t in (u, w, q_dec, k_dec, qk, tot))
    s_fin, o = lax.scan(step, s0, xs)
    return jnp.moveaxis(o, 0, 2).reshape(b, h, L, dv), s_fin


def _gla_chunked(q, k, v, logf, s0):
    b, h, L, _ = q.shape
    dv = v.shape[-1]
    n = L // CHUNK
    q, k, v, logf = (t.reshape(b, h, n, CHUNK, -1) for t in (q, k, v, logf))
    bc = jnp.cumsum(logf, axis=3)
    ref = bc[..., CHUNK // 2:CHUNK // 2 + 1, :]
    incl = jnp.tril(jnp.ones((CHUNK, CHUNK), bool))
    scores = jnp.einsum('bhnid,bhnjd->bhnij', q * jnp.exp(bc - ref), k * jnp.exp(ref - bc))
    scores = jnp.where(incl, scores, 0.0)
    o_intra = jnp.einsum('bhnij,bhnjv->bhniv', scores, v)
    q_dec = q * jnp.exp(bc)
    k_dec = k * jnp.exp(bc[..., -1:, :] - bc)
    tot = jnp.exp(bc[..., -1, :])

    def step(S, xs):
        qd_c, kd_c, v_c, tot_c = xs
        o_c = jnp.einsum('bhcd,bhdv->bhcv', qd_c, S)
        S = tot_c[..., :, None] * S + jnp.einsum('bhcd,bhcv->bhdv', kd_c, v_c)
        return S, o_c

    xs = tuple(jnp.moveaxis(t, 2, 0) for t in (q_dec, k_dec, v, tot))
    s_fin, o_inter = lax.scan(step, s0, xs)
    o = o_intra + jnp.moveaxis(o_inter, 0, 2)
    return o.reshape(b, h, L, dv), s_fin


def _gdn_branch(qkv, gate, ab, conv_w, a_log, dt_bias, norm_w, s0):
    f32 = jnp.float32
    dtype = qkv.dtype
    b, L, _ = qkv.shape
    qkv = jax.nn.silu(_short_conv(qkv, conv_w)).astype(f32)
    q, k, v = jnp.split(qkv, [A_HEADS * A_DK, 2 * A_HEADS * A_DK], axis=-1)
    q = _l2norm(_heads(q, A_HEADS)) * (A_DK ** -0.5)
    k = _l2norm(_heads(k, A_HEADS))
    v = _heads(v, A_HEADS)
    ab = ab.astype(f32).reshape(b, L, 4, A_HEADS).transpose(0, 2, 3, 1)
    g = -jnp.exp(a_log.astype(f32))[None, :, :, None] * jax.nn.softplus(
        ab[:, :2] + dt_bias.astype(f32)[None, :, :, None])
    beta = jax.nn.sigmoid(ab[:, 2:])
    o_f, s_f = _gdn_chunked(q, k, v, g[:, 0], beta[:, 0], s0[0])
    o_b, s_b = _gdn_chunked(_flip(q), _flip(k), _flip(v), _flip(g[:, 1]), _flip(beta[:, 1]), s0[1])
    o = o_f + _flip(o_b)
    o = _rmsnorm(o, norm_w) * jax.nn.silu(_heads(gate.astype(f32), A_HEADS))
    return _unheads(o).astype(dtype), (s_f, s_b)


def _hgrn2_branch(q, f2, i, gate, lb, norm_w, s0):
    f32 = jnp.float32
    dtype = q.dtype
    q = _heads(jax.nn.silu(q.astype(f32)), B_HEADS) * (B_DK ** -0.5)
    v = _heads(i.astype(f32), B_HEADS)
    lb = lb.astype(f32)

    def forget(fx):
        f = lb + (1 - lb) * jax.nn.sigmoid(fx)
        return _heads(jnp.log(f), B_HEADS), _heads(1 - f, B_HEADS)

    ff, fb = jnp.split(f2.astype(f32), 2, axis=-1)
    logf_f, k_f = forget(ff)
    logf_b, k_b = forget(fb)
    o_f, s_f = _gla_chunked(q, k_f, v, logf_f, s0[0])
    o_b, s_b = _gla_chunked(_flip(q), _flip(k_b), _flip(v), _flip(logf_b), s0[1])
    o = o_f + _flip(o_b)
    o = _rmsnorm(o, norm_w) * jax.nn.silu(_heads(gate.astype(f32), B_HEADS))
    return _unheads(o).astype(dtype), (s_f, s_b)


def _token_mixers(h, hc, need_ctx, w_in, conv_w, a_log, dt_bias, gdn_norm_w, lb, hg_norm_w, w_pa, w_pb, w_o):
    b, s, _ = h.shape
    rows = s // GRID_W
    qkv, ga, ab, qb, fb2, ib, gb, mg = jnp.split(h @ w_in, SPLIT_IDX, axis=-1)
    qkv_c, ga_c, ab_c, qb_c, fb2_c, ib_c, gb_c, mg_c = jnp.split(hc @ w_in, SPLIT_IDX, axis=-1)
    za = jnp.zeros((b, A_HEADS, A_DK, A_DV), jnp.float32)
    zb = jnp.zeros((b, B_HEADS, B_DK, B_DV), jnp.float32)
    oa_c, sa = _gdn_branch(qkv_c, ga_c, ab_c, conv_w, a_log, dt_bias, gdn_norm_w, (za, za))
    oa, _ = _gdn_branch(qkv, ga, ab, conv_w, a_log, dt_bias, gdn_norm_w, sa)
    to_col = lambda t: t.reshape(b, rows, GRID_W, -1).transpose(0, 2, 1, 3).reshape(b, s, -1)
    from_col = lambda t: t.reshape(b, GRID_W, rows, -1).transpose(0, 2, 1, 3).reshape(b, s, -1)
    ob_c, sb = _hgrn2_branch(qb_c, fb2_c, ib_c, gb_c, lb, hg_norm_w, (zb, zb))
    ob, _ = _hgrn2_branch(to_col(qb), to_col(fb2), to_col(ib), to_col(gb), lb, hg_norm_w, sb)
    ob = from_col(ob)

    def merge(o_a, o_b, gates):
        g_a, g_b = jnp.split(jax.nn.sigmoid(gates), 2, axis=-1)
        return (g_a * (o_a @ w_pa) + g_b * (o_b @ w_pb)) @ w_o

    mix = merge(oa, ob, mg)
    mix_c = merge(oa_c, ob_c, mg_c) if need_ctx else None
    return mix, mix_c


def _peer(h, w_query, sub_keys, expert_u, expert_v):
    b, s, d = h.shape
    n_tok = b * s
    t = h.reshape(n_tok, d)
    q = (t @ w_query).reshape(n_tok, P_HEADS, 2, P_DQ)
    sc = jnp.einsum('thcd,hckd->thck', q, sub_keys)
    s1, i1 = lax.top_k(sc[:, :, 0], P_TOPK)
    s2, i2 = lax.top_k(sc[:, :, 1], P_TOPK)
    cand = (s1[..., :, None] + s2[..., None, :]).reshape(n_tok, P_HEADS, P_TOPK * P_TOPK)
    top_s, top_i = lax.top_k(cand, P_TOPK)
    e1 = jnp.take_along_axis(i1, top_i // P_TOPK, axis=-1)
    e2 = jnp.take_along_axis(i2, top_i % P_TOPK, axis=-1)
    nb = n_tok // P_BLOCK
    experts = (e1 * P_NKEYS + e2).reshape(nb, P_BLOCK, P_HEADS * P_TOPK)
    gates = jax.nn.softmax(top_s.astype(jnp.float32), axis=-1).astype(h.dtype)
    gates = gates.reshape(nb, P_BLOCK, P_HEADS * P_TOPK)

    def block(args):
        tb, eb, gb = args
        act = jax.nn.gelu(jnp.einsum('td,tkd->tk', tb, expert_u[eb]), approximate=False) * gb
        return jnp.einsum('tk,tkd->td', act, expert_v[eb])

    y = lax.map(block, (t.reshape(nb, P_BLOCK, d), experts, gates))
    return y.reshape(b, s, d)


def setup_inputs(seed: int = 0) -> dict:
    key = jax.random.key(seed)
    ks = jax.random.split(key, 24)
    f32 = jnp.float32
    nrm = lambda k, shape, scale: jax.random.normal(k, shape, f32) * scale
    dt = jnp.exp(jax.random.uniform(ks[10], (DEPTH, 2, A_HEADS), f32, math.log(1e-3), math.log(1e-1)))
    return {
        'x': nrm(ks[0], (BATCH, SEQ, D_MODEL), 1.0),
        'c': nrm(ks[1], (BATCH, D_MODEL), 1.0),
        'ctx': nrm(ks[2], (BATCH, CTX_LEN, D_MODEL), 1.0),
        'c_ctx': nrm(ks[3], (D_MODEL,), 1.0),
        'w_ada': nrm(ks[4], (DEPTH, D_MODEL, 6 * D_MODEL), 0.5 * D_MODEL ** -0.5),
        'b_ada': nrm(ks[5], (DEPTH, 6 * D_MODEL), 0.02),
        'norm1_w': 1.0 + nrm(ks[6], (DEPTH, D_MODEL), 0.02),
        'w_in': nrm(ks[7], (DEPTH, D_MODEL, N_IN), D_MODEL ** -0.5),
        'conv_w': nrm(ks[8], (DEPTH, A_CONV, A_QKV), A_CONV ** -0.5),
        'a_log': jnp.log(jax.random.uniform(ks[9], (DEPTH, 2, A_HEADS), f32, 1.0, 16.0)),
        'dt_bias': dt + jnp.log(-jnp.expm1(-dt)),
        'gdn_norm_w': 1.0 + nrm(ks[11], (DEPTH, A_DV), 0.02),
        'lb_logits': nrm(ks[12], (DEPTH + 1, B_HEADS * B_DK), 0.1),
        'hg_norm_w': 1.0 + nrm(ks[13], (DEPTH, B_DV), 0.02),
        'w_pa': nrm(ks[14], (DEPTH, A_HEADS * A_DV, D_MODEL), (A_HEADS * A_DV) ** -0.5),
        'w_pb': nrm(ks[15], (DEPTH, B_HEADS * B_DV, D_MODEL), (B_HEADS * B_DV) ** -0.5),
        'w_o': nrm(ks[16], (DEPTH, D_MODEL, D_MODEL), D_MODEL ** -0.5),
        'norm2_w': 1.0 + nrm(ks[17], (DEPTH, D_MODEL), 0.02),
        'w_query': nrm(ks[18], (DEPTH, D_MODEL, P_HEADS * 2 * P_DQ), D_MODEL ** -0.5),
        'sub_keys': nrm(ks[19], (DEPTH, P_HEADS, 2, P_NKEYS, P_DQ), P_DQ ** -0.5),
        'expert_u': nrm(ks[20], (DEPTH, P_EXPERTS, D_MODEL), D_MODEL ** -0.5),
        'expert_v': nrm(ks[21], (DEPTH, P_EXPERTS, D_MODEL), D_MODEL ** -0.5),
        'final_norm_w': 1.0 + nrm(ks[22], (D_MODEL,), 0.02),
    }


def reference(x, c, ctx, c_ctx, w_ada, b_ada, norm1_w, w_in, conv_w, a_log, dt_bias, gdn_norm_w,
              lb_logits, hg_norm_w, w_pa, w_pb, w_o, norm2_w, w_query, sub_keys, expert_u, expert_v,
              final_norm_w):
    lb_all = jnp.cumsum(jax.nn.softmax(lb_logits.astype(jnp.float32), axis=0), axis=0)
    xc = ctx
    for l in range(DEPTH):
        last = l == DEPTH - 1
        mod = jax.nn.silu(c) @ w_ada[l] + b_ada[l]
        mod_c = jax.nn.silu(c_ctx) @ w_ada[l] + b_ada[l]
        sh1, sc1, g1, sh2, sc2, g2 = jnp.split(mod[:, None, :], 6, axis=-1)
        sh1c, sc1c, g1c, sh2c, sc2c, g2c = jnp.split(mod_c, 6, axis=-1)
        h = _modulate(_rmsnorm(x, norm1_w[l]), sh1, sc1)
        hc = _modulate(_rmsnorm(xc, norm1_w[l]), sh1c, sc1c)
        mix, mix_c = _token_mixers(h, hc, not last, w_in[l], conv_w[l], a_log[l], dt_bias[l], gdn_norm_w[l],
                                   lb_all[l], hg_norm_w[l], w_pa[l], w_pb[l], w_o[l])
        x = x + g1 * mix
        x = x + g2 * _peer(_modulate(_rmsnorm(x, norm2_w[l]), sh2, sc2),
                           w_query[l], sub_keys[l], expert_u[l], expert_v[l])
        if not last:
            xc = xc + g1c * mix_c
            xc = xc + g2c * _peer(_modulate(_rmsnorm(xc, norm2_w[l]), sh2c, sc2c),
                                  w_query[l], sub_keys[l], expert_u[l], expert_v[l])
    return _rmsnorm(x, final_norm_w)
```

```python
from contextlib import ExitStack
import numpy as np
import concourse.bass as bass
import concourse.mybir as mybir
from concourse.bass_utils import run_bass_kernel_spmd

F32 = mybir.dt.float32
BF16 = mybir.dt.bfloat16
AF = mybir.ActivationFunctionType
ALU = mybir.AluOpType
AX = mybir.AxisListType


class Buf:
    __slots__ = ("name", "w", "r")

    def __init__(self, name=""):
        self.name = name
        self.w = None
        self.r = []


class T:
    def __init__(self, fw, tensor, name):
        self.t = tensor
        self.buf = Buf(name)
        self.name = name

    def __getitem__(self, key):
        return V(self.t[key], self.buf)

    def ap(self):
        return self.t.ap() if hasattr(self.t, "ap") and callable(getattr(self.t, "ap")) else self.t[:]


class V:
    __slots__ = ("ap", "buf")

    def __init__(self, ap, buf):
        self.ap = ap
        self.buf = buf

    def __getitem__(self, key):
        return V(self.ap[key], self.buf)

    def re(self, s, **kw):
        return V(self.ap.rearrange(s, **kw), self.buf)

    def bc(self, shape):
        return V(self.ap.to_broadcast(shape), self.buf)

    def with_ap(self, ap):
        return V(ap, self.buf)


def _ap(x):
    return x.ap if hasattr(x, "buf") or hasattr(x, "bufs") else x


class FW:
    ENG = ("pe", "act", "dve", "pool", "sp")

    def __init__(self, nc, stack):
        self.nc = nc
        self.stack = stack
        self.root = stack
        self.free_dsems = []
        self.phase_dsems = []
        self.h = {"pe": nc.tensor, "act": nc.scalar, "dve": nc.vector, "pool": nc.gpsimd, "sp": nc.sync}
        self.lists = {e: [] for e in self.ENG}
        self.EPOCH = 16000
        self.semobjs = {}
        self.cur = {}
        for e in ("pe", "act", "dve", "pool"):
            self._new_epoch(e, 0)
        self.cnt = {e: 0 for e in self.ENG}
        self.known = {e: {} for e in self.ENG}
        self.dsems = []
        self.n_instr = 0
        self.n_wait = 0
        self._dma_rr = 0
        self.uid = 0
        self.last = {}
        self.lastd = {}

    def _new_epoch(self, e, n):
        if not hasattr(self, "snaps"):
            self.snaps = {x: [] for x in ("pe", "act", "dve", "pool")}
            self._wsrc = {}
        key = f"{e}{n}"
        self._wsrc[key] = e
        sem = self.root.enter_context(self.nc.semaphore("s_" + key))
        self.semobjs[key] = sem
        self.cur[e] = [key, sem, 0, n]

    def sb(self, shape, dt, name=None):
        self.uid += 1
        name = name or f"sb{self.uid}"
        t = self.stack.enter_context(self.nc.sbuf_tensor(name, list(shape), dt))
        return T(self, t, name)

    def ps(self, shape, dt=F32, name=None):
        self.uid += 1
        name = name or f"ps{self.uid}"
        t = self.stack.enter_context(self.nc.psum_tensor(name, list(shape), dt))
        return T(self, t, name)

    def dsem(self, name=None):
        if self.free_dsems:
            ent = self.free_dsems.pop()
        else:
            self.uid += 1
            key = f"d{self.uid}"
            s = self.root.enter_context(self.nc.semaphore(key))
            ent = [s, 0, key]
            self.semobjs[key] = s
            self.n_sems = getattr(self, "n_sems", 0) + 1
        self.phase_dsems.append(ent)
        return ent

    def phase_end(self):
        self.barrier()
        self.free_dsems.extend(self.phase_dsems)
        self.phase_dsems = []

    def _need(self, eng, ev, waits):
        if ev is None:
            return
        key, val = ev[0], ev[1]
        if eng == "pe" and ev[2] == "pe":
            return
        if self.known[eng].get(key, 0) >= val:
            return
        if waits.get(key, 0) < val:
            waits[key] = val

    def _gidx(self, key, val, e):
        return int(key[len(e):]) * self.EPOCH + val

    def _emit_waits(self, eng, waits, semobjs):
        if not waits:
            return
        kn = self.known[eng]
        for key, val in waits.items():
            sem = semobjs[key]
            kn[key] = val
            self.n_wait += 1
            self.lists[eng].append(("w", sem, val))
            src = self._wsrc.get(key)
            if src is not None:
                snaps = self.snaps[src]
                g = self._gidx(key, val, src)
                lo, hi = 0, len(snaps)
                while lo < hi:
                    mid = (lo + hi) // 2
                    if snaps[mid][0] < g:
                        lo = mid + 1
                    else:
                        hi = mid
                if lo > 0:
                    for k2, v2 in snaps[lo - 1][1].items():
                        if kn.get(k2, 0) < v2:
                            kn[k2] = v2
        if eng in self.snaps:
            c = self.cur[eng]
            self.snaps[eng].append((c[3] * self.EPOCH + c[2], dict(kn)))

    def op(self, eng, fn, reads=(), writes=()):
        waits = {}
        semobjs = self._semobjs
        rb = [x.buf if isinstance(x, (V, T)) else x for x in reads]
        wb = [x.buf if isinstance(x, (V, T)) else x for x in writes]
        for b in rb:
            self._need(eng, b.w, waits)
        for b in wb:
            self._need(eng, b.w, waits)
            for ev in b.r:
                self._need(eng, ev, waits)
        self._emit_waits(eng, waits, semobjs)
        c = self.cur[eng]
        if c[2] >= self.EPOCH:
            self._new_epoch(eng, c[3] + 1)
            c = self.cur[eng]
        c[2] += 1
        ev = (c[0], c[2], eng)
        self.lists[eng].append(("i", fn, c[1]))
        self.last[eng] = ev
        self.n_instr += 1
        for b in wb:
            b.w = ev
            b.r = []
        for b in rb:
            if b in wb:
                continue
            b.r = [e for e in b.r if e[2] != eng] + [ev]
        return ev

    @property
    def _semobjs(self):
        return self.semobjs

    def dma(self, out, in_, dsem, queue="sp", reads=None, writes=None, **kw):
        eng = queue
        waits = {}
        rb = [x.buf for x in (reads if reads is not None else [in_])]
        wb = [x.buf for x in (writes if writes is not None else [out])]
        for b in rb:
            self._need(eng, b.w, waits)
        for b in wb:
            self._need(eng, b.w, waits)
            for ev in b.r:
                self._need(eng, ev, waits)
        self._emit_waits(eng, waits, self._semobjs)
        if dsem[1] >= self.EPOCH:
            self.uid += 1
            key = f"d{self.uid}"
            dsem[0] = self.root.enter_context(self.nc.semaphore(key))
            dsem[1] = 0
            dsem[2] = key
            self.semobjs[key] = dsem[0]
        dsem[1] += 16
        ev = (dsem[2], dsem[1], None)
        o_ap, i_ap = _ap(out), _ap(in_)
        self.lists[eng].append(("d", o_ap, i_ap, dsem[0], kw))
        self.lastd[dsem[2]] = ev
        self.n_instr += 1
        for b in wb:
            b.w = ev
            b.r = []
        for b in rb:
            if b in wb:
                continue
            b.r = b.r + [ev]
        return ev

    def wait_all(self, eng, evs):
        waits = {}
        for ev in evs:
            self._need(eng, ev, waits)
        self._emit_waits(eng, waits, self._semobjs)

    def barrier(self):
        evs = list(self.last.values()) + list(self.lastd.values())
        for e in self.ENG:
            self.wait_all(e, evs)
        self.lastd = {}

    def replay(self, block):
        lists = self.lists

        def run(eng_handle, items):
            for it in items:
                if it[0] == "w":
                    eng_handle.wait_ge(it[1], it[2])
                elif it[0] == "i":
                    it[1](eng_handle).then_inc(it[2], 1)
                else:
                    _, o, i, s, kw = it
                    eng_handle.dma_start(out=o, in_=i, **kw).then_inc(s, 16)

        @block.tensor
        def _(e):
            run(e, lists["pe"])

        @block.scalar
        def _(e):
            run(e, lists["act"])

        @block.vector
        def _(e):
            run(e, lists["dve"])

        @block.gpsimd
        def _(e):
            run(e, lists["pool"])

        @block.sync
        def _(e):
            run(e, lists["sp"])


def _bufs(*xs):
    return [x for x in xs if isinstance(x, (V, T))]


def mm(fw, out, lhsT, rhs, start=True, stop=True):
    o, l, r = out.ap, lhsT.ap, rhs.ap
    return fw.op("pe", lambda e: e.matmul(o, lhsT=l, rhs=r, start=start, stop=stop), [lhsT, rhs], [out])


def tr(fw, out, in_, ident):
    o, i, d = out.ap, in_.ap, ident.ap
    return fw.op("pe", lambda e: e.transpose(o, i, d), [in_, ident], [out])


def act(fw, out, in_, func, bias=None, scale=1.0, accum=None, eng="act"):
    o, i = out.ap, in_.ap
    kw = {}
    reads = [in_]
    writes = [out]
    if bias is not None:
        kw["bias"] = _ap(bias)
        reads += _bufs(bias)
    if isinstance(scale, V):
        reads.append(scale)
    kw["scale"] = _ap(scale)
    if accum is not None:
        kw["accum_out"] = accum.ap
        writes.append(accum)
    return fw.op("act", lambda e: e.activation(out=o, in_=i, func=func, **kw), reads, writes)


def tt(fw, eng, out, in0, in1, op):
    o, a, b = out.ap, in0.ap, in1.ap
    return fw.op(eng, lambda e: e.tensor_tensor(out=o, in0=a, in1=b, op=op), [in0, in1], [out])


def ts(fw, eng, out, in0, s1, s2, op0, op1=None, accum=None):
    o, a = out.ap, in0.ap
    kw = {}
    if op1 is not None:
        kw["op1"] = op1
    writes = [out]
    if accum is not None:
        kw["accum_out"] = accum.ap
        writes.append(accum)
    return fw.op(eng, lambda e: e.tensor_scalar(out=o, in0=a, scalar1=_ap(s1), scalar2=_ap(s2), op0=op0, **kw),
                 [in0] + _bufs(s1, s2), writes)


def stt(fw, eng, out, in0, scalar, in1, op0, op1):
    o, a, b = out.ap, in0.ap, in1.ap
    return fw.op(eng, lambda e: e.scalar_tensor_tensor(out=o, in0=a, scalar=_ap(scalar), in1=b, op0=op0, op1=op1),
                 [in0, in1] + _bufs(scalar), [out])


def cp(fw, eng, out, in_):
    o, i = out.ap, in_.ap
    if eng == "act":
        return fw.op("act", lambda e: e.activation(out=o, in_=i, func=AF.Copy), [in_], [out])
    return fw.op(eng, lambda e: e.tensor_copy(out=o, in_=i), [in_], [out])


def memset(fw, eng, out, val):
    o = out.ap
    return fw.op(eng, lambda e: e.memset(o, val), [], [out])


def recip(fw, out, in_):
    o, i = out.ap, in_.ap
    return fw.op("dve", lambda e: e.reciprocal(out=o, in_=i), [in_], [out])


D = 1024
EPS = 1e-6
U8 = mybir.dt.uint8


class DR:
    def __init__(self, nc, name, shape, dt, npieces=1, kind=None):
        if kind is None:
            self.t = nc.dram_tensor(name, list(shape), dt)
        else:
            self.t = nc.dram_tensor(name, list(shape), dt, kind=kind)
        self.bufs = [Buf(f"{name}{i}") for i in range(npieces)]
        self.all = self.bufs

    def v(self, ap, piece=0):
        return V(ap, self.bufs[piece])

    def ap(self):
        return self.t.ap()


class MV:
    def __init__(self, ap, bufs):
        self.ap = ap
        self.bufs = bufs


def build(W, stages=("all",), dbg=()):
    nc = bass.Bass("TRN2", target_bir_lowering=False)
    NT = W + 2
    NTOK = 128 * NT
    S = 128 * W
    SH = S // 2
    inp = {}

    def din(name, shape, dt=F32):
        inp[name] = DR(nc, name, shape, dt, kind="ExternalInput")
        return inp[name]

    xa = din("xa", [NTOK, D])
    ccol = din("ccol", [128, 16])
    w_ada = din("w_ada", [D, 6 * D]); b_adaT = din("b_adaT", [128, 48]); b_row = din("b_row", [1, 6 * D])
    n1 = din("n1", [128, 8]); n2 = din("n2", [128, 8]); fnw = din("fnw", [128, D])
    wA = din("wA", [D, 1536]); wGA = din("wGA", [D, 512]); wAB = din("wAB", [D, 16])
    wBq = din("wBq", [D, 512]); wBf = din("wBf", [D, 1024]); wBv = din("wBv", [D, 512]); wBg = din("wBg", [D, 512])
    wMG = din("wMG", [D, 2048])
    convw = din("convw", [128, 36]); gpar = din("gpar", [128, 16])
    gnw = din("gnw", [128, 128]); hnw = din("hnw", [128, 128]); lbl = din("lbl", [128, 8])
    xh = din("xh", [SH, D]); hsel = din("hsel", [128, 2])
    if "merge" in stages or "all" in stages:
        w_pa = din("w_pa", [D, D]); w_pb = din("w_pb", [D, D]); w_o = din("w_o", [D, D])
    if "peer" in stages or "all" in stages:
        w_q = din("w_q", [D, 2048]); skT = din("skT", [128, 16 * 128])
        uT = din("uT", [D, 16384]); ev = din("ev", [16384, D])
    out = DR(nc, "out", [SH, D], F32, kind="ExternalOutput")
    dbg_t = {}

    QT_d = DR(nc, "QT_d", [4, 128, NTOK], BF16, NT)
    KT_d = DR(nc, "KT_d", [4, 128, NTOK], BF16, NT)
    Ktm_d = DR(nc, "Ktm_d", [NTOK, 512], BF16, NT)
    Vtm_d = DR(nc, "Vtm_d", [NTOK, 512], BF16, NT)
    GA_d = DR(nc, "GA_d", [NTOK, 512], BF16, NT)
    GB_d = DR(nc, "GB_d", [NTOK, 16], F32, NT)
    OA_d = [DR(nc, f"OA_d{d}", [NTOK, 512], F32, NT) for d in range(2)]
    QE_d = [DR(nc, f"QE_d{d}", [4, 128, NTOK], BF16, NT) for d in range(2)]
    KE_d = [DR(nc, f"KE_d{d}", [4, 128, NTOK], BF16, NT) for d in range(2)]
    QD_d = [DR(nc, f"QD_d{d}", [4, 128, NTOK], BF16, NT) for d in range(2)]
    KDtm_d = [DR(nc, f"KDtm_d{d}", [NTOK, 512], BF16, NT) for d in range(2)]
    VB_d = DR(nc, "VB_d", [NTOK, 512], BF16, NT)
    GBG_d = DR(nc, "GBG_d", [NTOK, 512], BF16, NT)
    OB_d = [DR(nc, f"OB_d{d}", [NTOK, 512], F32, NT) for d in range(2)]
    XCH_d = DR(nc, "XCH_d", [S, 1024], BF16, 2 * W)
    XG_d = DR(nc, "XG_d", [2 * S, 1024], BF16, 1)
    BAR_s = DR(nc, "BAR_s", [64, 128], BF16, 1); BAR_g = DR(nc, "BAR_g", [128, 128], BF16, 1)
    NTL = SH // 128
    X1_d = DR(nc, "X1_d", [SH, D], F32, NTL)
    H2T_d = DR(nc, "H2T_d", [NTL, 128, D], BF16, NTL)
    SS_d = DR(nc, "SS_d", [SH, 2048], F32, NTL)
    ST_d = DR(nc, "ST_d", [SH, 16], F32, NTL)

    st = ExitStack()
    with st:
        fw = FW(nc, st)
        block = st.enter_context(nc.Block())

        def dbg_out(name, src_v, shape, dt=F32):
            if name not in dbg:
                return
            t = DR(nc, "dbg_" + name, shape, dt, kind="ExternalOutput")
            dbg_t[name] = t
            fw.dma(t.v(t.ap()), src_v, fw.dsem(), queue="sp")

        def dbg_dram(name, dr, shape, dt):
            if name not in dbg:
                return
            t = DR(nc, "dbg_" + name, shape, dt, kind="ExternalOutput")
            dbg_t[name] = t
            fw.dma(t.v(t.ap()), V(dr.ap(), dr.bufs[0]), fw.dsem(), queue="sp", reads=[V(None, b) for b in dr.bufs])

        ident_bf = fw.sb([128, 128], BF16); ident_f = fw.sb([128, 128], F32); ones_f = fw.sb([128, 128], F32)
        mF = fw.sb([128, 128], F32); mB = fw.sb([128, 128], F32); mFs = fw.sb([128, 128], F32); mBs = fw.sb([128, 128], F32)

        def sel(t, pat, cm, op):
            memset(fw, "pool", t[:], 1.0)
            o = t.t[:]
            fw.op("pool", lambda e: e.affine_select(out=o, in_=o, pattern=[[pat, 128]], compare_op=op, fill=0.0,
                                                    base=0, channel_multiplier=cm), [t], [t])
        sel(ident_f, -1, 1, ALU.is_equal)
        cp(fw, "pool", ident_bf[:], ident_f[:])
        memset(fw, "pool", ones_f[:], 1.0)
        sel(mF, 1, -1, ALU.is_ge)
        sel(mB, -1, 1, ALU.is_ge)
        sel(mFs, 1, -1, ALU.is_gt)
        sel(mBs, -1, 1, ALU.is_gt)
        MASK = [mF, mB]; MASKS = [mFs, mBs]
        eps_col = fw.sb([128, 1], F32); one_col = fw.sb([128, 1], F32)
        memset(fw, "pool", eps_col[:], EPS); memset(fw, "pool", one_col[:], 1.0)

        def load_small(dr, shape, dt=F32, queue="sp"):
            t = fw.sb(shape, dt)
            fw.dma(t[:], dr.v(dr.ap()), fw.dsem(), queue=queue)
            return t
        n1_s = load_small(n1, [128, 8]); n2_s = load_small(n2, [128, 8])
        convw_s = load_small(convw, [128, 36]); gpar_s = load_small(gpar, [128, 16])
        gnw_s = load_small(gnw, [128, 128]); hnw_s = load_small(hnw, [128, 128]); lbl_s = load_small(lbl, [128, 8])
        badaT_s = load_small(b_adaT, [128, 48]); ccol_s = load_small(ccol, [128, 16])

        modT = fw.sb([128, 48, 2], F32)
        g1_bc = fw.sb([128, D], F32); g2_bc = fw.sb([128, D], F32)
        A1 = fw.sb([128, 8, 2], F32); A2 = fw.sb([128, 8], F32)
        with ExitStack() as ph:
            fw.stack = ph
            sc = fw.sb([128, 16], F32)
            act(fw, sc[:], ccol_s[:], AF.Silu)
            scv = sc[:].re("p (k w) -> p k w", w=2)
            wts = [fw.sb([128, 8, 768], F32) for _ in range(2)]
            wsem = [fw.dsem() for _ in range(2)]
            psm = fw.ps([128, 96])
            wv = w_ada.ap().rearrange("(k p) n -> p k n", p=128)
            for blk in range(8):
                wt = wts[blk % 2]
                fw.dma(wt[:], w_ada.v(wv[:, :, blk * 768:(blk + 1) * 768]), wsem[blk % 2])
                for jj in range(6):
                    j = blk * 6 + jj
                    for k in range(8):
                        mm(fw, psm[:, 2 * j:2 * j + 2], wt[:, k, jj * 128:(jj + 1) * 128], scv[:, k, :], k == 0, k == 7)
            tt(fw, "dve", modT[:], psm[:].re("p (j w) -> p j w", w=2), badaT_s[:].re("p (j o) -> p j o", o=1).bc([128, 48, 2]), ALU.add)
            stt(fw, "dve", A1[:], modT[:, 8:16, :], 1.0, n1_s[:].re("p (k o) -> p k o", o=1).bc([128, 8, 2]), ALU.add, ALU.mult)
            stt(fw, "dve", A2[:], modT[:, 32:40, 0], 1.0, n2_s[:], ALU.add, ALU.mult)
            brow = fw.sb([1, 2 * D], F32)
            fw.dma(brow[:, 0:D], b_row.v(b_row.ap()[:, 2 * D:3 * D]), fw.dsem())
            fw.dma(brow[:, D:2 * D], b_row.v(b_row.ap()[:, 5 * D:6 * D]), fw.dsem())
            grow = fw.sb([2, 2 * D], F32)
            wg = fw.sb([128, 8, D], F32)
            wgs = fw.dsem()
            psr = fw.ps([128, 512])
            for gi, c0 in enumerate((2 * D, 5 * D)):
                fw.dma(wg[:], w_ada.v(wv[:, :, c0:c0 + D]), wgs)
                for hb in range(2):
                    for k in range(8):
                        mm(fw, psr[0:2, :], scv[:, k, :], wg[:, k, hb * 512:(hb + 1) * 512], k == 0, k == 7)
                    tt(fw, "dve", grow[0:1, gi * D + hb * 512: gi * D + (hb + 1) * 512], psr[0:1, :],
                       brow[0:1, gi * D + hb * 512: gi * D + (hb + 1) * 512], ALU.add)
            for gi, gbc in enumerate((g1_bc, g2_bc)):
                for hb in range(2):
                    mm(fw, psr[:], ones_f[0:1, :], grow[0:1, gi * D + hb * 512: gi * D + (hb + 1) * 512])
                    cp(fw, "act", gbc[:, hb * 512:(hb + 1) * 512], psr[:])
            dbg_out("modT", modT[:], [128, 48, 2])
            dbg_out("g1bc", g1_bc[:], [128, D])
            fw.phase_end()
        fw.stack = st

        def make_hT(xsrc_v, hT_dst, which, A, Sh, xt, junk, ssq, rstd, xn, pst):
            fw.dma(xt[:], xsrc_v, xt_sem[id(xt)])
            act(fw, junk[:], xt[:], AF.Square, accum=ssq[:])
            act(fw, rstd[:], ssq[:], AF.Sqrt, bias=eps_col[:], scale=1.0 / D)
            recip(fw, rstd[:], rstd[:])
            ts(fw, "dve", xn[:], xt[:], rstd[:], None, ALU.mult)
            for k in range(8):
                tr(fw, pst[:, k * 128:(k + 1) * 128], xn[:, k * 128:(k + 1) * 128], ident_bf[:])
            pv = pst[:].re("p (k t) -> p k t", k=8)
            tt(fw, "dve", hT_dst, pv, A.re("p (k o) -> p k o", o=1).bc([128, 8, 128]), ALU.mult)
            tt(fw, "pool", hT_dst, hT_dst, Sh.re("p (k o) -> p k o", o=1).bc([128, 8, 128]), ALU.add)

        xt_sem = {}

        if "aprep" in stages or "all" in stages:
            with ExitStack() as ph:
                fw.stack = ph
                WA = fw.sb([128, 8, 1536], BF16); WGA = fw.sb([128, 8, 512], BF16); WAB = fw.sb([128, 8, 16], BF16)
                fw.dma(WA[:], wA.v(wA.ap().rearrange("(k p) n -> p k n", p=128)), fw.dsem(), queue="pool")
                fw.dma(WGA[:], wGA.v(wGA.ap().rearrange("(k p) n -> p k n", p=128)), fw.dsem(), queue="pool")
                fw.dma(WAB[:], wAB.v(wAB.ap().rearrange("(k p) n -> p k n", p=128)), fw.dsem(), queue="pool")
                xts = [fw.sb([128, D], F32) for _ in range(2)]
                for x_ in xts:
                    xt_sem[id(x_)] = fw.dsem()
                junk = fw.sb([128, D], BF16); ssq = fw.sb([128, 1], F32); rstd = fw.sb([128, 1], F32)
                xn = fw.sb([128, D], BF16)
                pst = [fw.ps([128, D], BF16) for _ in range(2)]
                hT = [fw.sb([128, 8, 512], BF16) for _ in range(2)]
                RAW = [fw.sb([128, 12, 514], F32) for _ in range(3)]
                CV = fw.sb([128, 12, 512], F32)
                SQ = fw.sb([128, 512], F32); RN = fw.sb([128, 512], F32)
                QK16 = fw.sb([128, 8, 512], BF16); V16 = fw.sb([128, 4, 512], BF16)
                psq = [fw.ps([128, 512]) for _ in range(2)]
                psg = fw.ps([128, 512])
                psab = fw.ps([128, 16])
                pstm = fw.ps([128, 512], BF16)
                ga_s = fw.sb([128, 512], BF16); gb_s = fw.sb([128, 16], F32); tmp16 = fw.sb([128, 16], F32)
                ktm_s = fw.sb([128, 512], BF16); vtm_s = fw.sb([128, 512], BF16)
                sem_st = {k: fw.dsem() for k in ("ga", "gb", "k", "v", "q", "kt")}
                negA = fw.sb([128, 8], F32)
                act(fw, negA[:], gpar_s[:, 8:16], AF.Exp)
                ts(fw, "dve", negA[:], negA[:], -1.0, None, ALU.mult)

                groups = [(0, 0, 2)] + [(1, 2 + 4 * g, 4) for g in range(W // 4)]
                tcount = 0
                qcount = 0

                def finish_group(gi, zero_next):
                    nonlocal qcount
                    seq, t0, ntl = groups[gi]
                    n = 128 * ntl
                    R = RAW[gi % 3]
                    if zero_next:
                        memset(fw, "pool", R[:, :, n + 1:n + 2], 0.0)
                    cwv = convw_s[:].re("p (m j) -> p m j", j=3)
                    for m in range(12):
                        if m % 3 != 2:
                            ts(fw, "dve", CV[:, m, 0:n], R[:, m, 0:n], cwv[:, m, 0:1], None, ALU.mult)
                            stt(fw, "dve", CV[:, m, 0:n], R[:, m, 1:n + 1], cwv[:, m, 1:2], CV[:, m, 0:n], ALU.mult, ALU.add)
                            stt(fw, "dve", CV[:, m, 0:n], R[:, m, 2:n + 2], cwv[:, m, 2:3], CV[:, m, 0:n], ALU.mult, ALU.add)
                        else:
                            ts(fw, "pool", CV[:, m, 0:n], R[:, m, 0:n], cwv[:, m, 0:1], None, ALU.mult)
                            for j_ in (1, 2):
                                ts(fw, "pool", SQ[:, 0:n], R[:, m, j_:n + j_], cwv[:, m, j_:j_ + 1], None, ALU.mult)
                                tt(fw, "pool", CV[:, m, 0:n], CV[:, m, 0:n], SQ[:, 0:n], ALU.add)
                    act(fw, CV[:, :, 0:n], CV[:, :, 0:n], AF.Silu)
                    for m in range(8):
                        tt(fw, "pool", SQ[:, 0:n], CV[:, m, 0:n], CV[:, m, 0:n], ALU.mult)
                        pq = psq[qcount % 2]; qcount += 1
                        mm(fw, pq[:, 0:n], ones_f[:], SQ[:, 0:n])
                        act(fw, RN[:, 0:n], pq[:, 0:n], AF.Sqrt, bias=eps_col[:])
                        recip(fw, RN[:, 0:n], RN[:, 0:n])
                        scl = (128.0 ** -0.5) if m < 4 else 1.0
                        stt(fw, "dve", QK16[:, m, 0:n], CV[:, m, 0:n], scl, RN[:, 0:n], ALU.mult, ALU.mult)
                    cp(fw, "act", V16[:, :, 0:n], CV[:, 8:12, 0:n])
                    tok0 = 128 * t0
                    fw.dma(MV(QT_d.ap().rearrange("h d t -> d h t")[:, :, tok0:tok0 + n], QT_d.bufs[t0:t0 + ntl]),
                           QK16[:, 0:4, 0:n], sem_st["q"], writes=[V(None, b) for b in QT_d.bufs[t0:t0 + ntl]])
                    fw.dma(MV(KT_d.ap().rearrange("h d t -> d h t")[:, :, tok0:tok0 + n], KT_d.bufs[t0:t0 + ntl]),
                           QK16[:, 4:8, 0:n], sem_st["kt"], writes=[V(None, b) for b in KT_d.bufs[t0:t0 + ntl]])
                    for tl in range(ntl):
                        for (src, mb, dst_s, dst_d, key) in ((QK16, 4, ktm_s, Ktm_d, "k"), (V16, 0, vtm_s, Vtm_d, "v")):
                            for h in range(4):
                                tr(fw, pstm[:, h * 128:(h + 1) * 128], src[:, mb + h, tl * 128:(tl + 1) * 128], ident_bf[:])
                            cp(fw, "act", dst_s[:], pstm[:])
                            tk = t0 + tl
                            fw.dma(dst_d.v(dst_d.ap()[tk * 128:(tk + 1) * 128, :], tk), dst_s[:], sem_st[key])

                for gi, (seq, t0, ntl) in enumerate(groups):
                    n = 128 * ntl
                    h_ = hT[gi % 2]
                    R = RAW[gi % 3]
                    for tl in range(ntl):
                        tk = t0 + tl
                        xt = xts[tcount % 2]; ps_ = pst[tcount % 2]; tcount += 1
                        w = 1 if seq == 0 else 0
                        make_hT(xa.v(xa.ap()[tk * 128:(tk + 1) * 128, :]), h_[:, :, tl * 128:(tl + 1) * 128], w,
                                A1[:, :, w], modT[:, 0:8, w], xt, junk, ssq, rstd, xn, ps_)
                        for k in range(8):
                            mm(fw, psg[:], h_[:, k, tl * 128:(tl + 1) * 128], WGA[:, k, :], k == 0, k == 7)
                        act(fw, ga_s[:], psg[:], AF.Silu)
                        fw.dma(GA_d.v(GA_d.ap()[tk * 128:(tk + 1) * 128, :], tk), ga_s[:], sem_st["ga"])
                        for k in range(8):
                            mm(fw, psab[:], h_[:, k, tl * 128:(tl + 1) * 128], WAB[:, k, :], k == 0, k == 7)
                        tt(fw, "dve", tmp16[:, 0:8], psab[:, 0:8], gpar_s[:, 0:8], ALU.add)
                        act(fw, tmp16[:, 0:8], tmp16[:, 0:8], AF.Exp)
                        act(fw, tmp16[:, 0:8], tmp16[:, 0:8], AF.Ln, bias=one_col[:])
                        tt(fw, "dve", gb_s[:, 0:8], tmp16[:, 0:8], negA[:], ALU.mult)
                        act(fw, gb_s[:, 8:16], psab[:, 8:16], AF.Sigmoid)
                        fw.dma(GB_d.v(GB_d.ap()[tk * 128:(tk + 1) * 128, :], tk), gb_s[:], sem_st["gb"])
                    for m in range(12):
                        pq = psq[qcount % 2]; qcount += 1
                        for k in range(8):
                            mm(fw, pq[:, 0:n], WA[:, k, m * 128:(m + 1) * 128], h_[:, k, 0:n], k == 0, k == 7)
                        cp(fw, "act" if m % 2 else "dve", R[:, m, 1:n + 1], pq[:, 0:n])
                    first_in_seq = (gi == 0) or (groups[gi - 1][0] != seq)
                    last_in_seq = (gi == len(groups) - 1) or (groups[gi + 1][0] != seq)
                    if first_in_seq:
                        memset(fw, "pool", R[:, :, 0:1], 0.0)
                    else:
                        Rp = RAW[(gi - 1) % 3]
                        npv = 128 * groups[gi - 1][2]
                        cp(fw, "pool", Rp[:, :, npv + 1:npv + 2], R[:, :, 1:2])
                        cp(fw, "pool", R[:, :, 0:1], Rp[:, :, npv:npv + 1])
                        finish_group(gi - 1, False)
                    if last_in_seq:
                        finish_group(gi, True)
                dbg_out("hT0", hT[0][:], [128, 8, 512], BF16)
                fw.phase_end()
                for nm, dr, sh, dt in (("QT", QT_d, [4, 128, NTOK], BF16), ("KT", KT_d, [4, 128, NTOK], BF16),
                                       ("Ktm", Ktm_d, [NTOK, 512], BF16), ("Vtm", Vtm_d, [NTOK, 512], BF16),
                                       ("GA", GA_d, [NTOK, 512], BF16), ("GB", GB_d, [NTOK, 16], F32)):
                    dbg_dram(nm, dr, sh, dt)
            fw.stack = st

        if "achain" in stages or "all" in stages:
            with ExitStack() as ph:
                fw.stack = ph
                order = [list(range(NT)), [1, 0] + list(range(NT - 1, 1, -1))]
                S32 = [fw.sb([128, 4, 128], F32) for _ in range(2)]
                S16 = [fw.sb([128, 4, 128], BF16) for _ in range(2)]
                for d_ in range(2):
                    memset(fw, "pool", S32[d_][:], 0.0)
                    memset(fw, "pool", S16[d_][:], 0.0)
                NB = 2
                qT_s = [[fw.sb([128, 4, 128], BF16) for _ in range(NB)] for _ in range(2)]
                kT_s = [[fw.sb([128, 4, 128], BF16) for _ in range(NB)] for _ in range(2)]
                v_s = [[fw.sb([128, 4, 128], BF16) for _ in range(NB)] for _ in range(2)]
                k_s = [[fw.sb([128, 4, 128], BF16) for _ in range(NB)] for _ in range(2)]
                gb_l = [[fw.sb([128, 16], F32) for _ in range(NB)] for _ in range(2)]
                lsem = [[[fw.dsem() for _ in range(5)] for _ in range(NB)] for _ in range(2)]
                gU = fw.sb([128, 4, 128], F32)
                psA = fw.ps([128, 512])
                psS = fw.ps([128, 16])
                psK = [fw.ps([128, 512]) for _ in range(2)]
                psI = [fw.ps([128, 512]) for _ in range(2)]
                psC = [fw.ps([128, 512]) for _ in range(2)]
                ngc = fw.sb([128, 4], F32); Gtm = fw.sb([128, 4], F32); nG = fw.sb([128, 4], F32)
                rem = fw.sb([128, 4], F32); krem = fw.sb([128, 4], F32); tot = fw.sb([128, 4], F32)
                Gbc = fw.sb([128, 4, 128], F32)
                E = fw.sb([128, 4, 128], F32); DTm = fw.sb([128, 4, 128], F32); DTs = fw.sb([128, 4, 128], F32)
                qdT = fw.sb([128, 4, 128], BF16); KD = fw.sb([128, 4, 128], BF16)
                Af = fw.sb([128, 4, 128], F32); ATf = fw.sb([128, 4, 128], F32)
                X = [fw.sb([128, 4, 128], F32) for _ in range(2)]; XT = [fw.sb([128, 4, 128], F32) for _ in range(2)]
                P = [fw.sb([128, 4, 128], F32) for _ in range(2)]
                P16 = fw.sb([128, 4, 128], BF16); QKT = fw.sb([128, 4, 128], BF16)
                Rr = fw.sb([128, 4, 128], BF16); VN = fw.sb([128, 4, 128], BF16)
                Osb = [fw.sb([128, 4, 128], F32) for _ in range(2)]
                osem = [fw.dsem() for _ in range(2)]
                identb4 = ident_f[:].re("p (o i) -> p o i", o=1).bc([128, 4, 128])

                def loads(s, d_):
                    cid = order[d_][s]
                    b = s % NB
                    tk = slice(cid * 128, (cid + 1) * 128)
                    fw.dma(qT_s[d_][b][:], QT_d.v(QT_d.ap().rearrange("h d t -> d h t")[:, :, tk], cid), lsem[d_][b][0])
                    fw.dma(kT_s[d_][b][:], KT_d.v(KT_d.ap().rearrange("h d t -> d h t")[:, :, tk], cid), lsem[d_][b][1])
                    fw.dma(v_s[d_][b][:].re("p h d -> p (h d)"), Vtm_d.v(Vtm_d.ap()[tk, :], cid), lsem[d_][b][2])
                    fw.dma(k_s[d_][b][:].re("p h d -> p (h d)"), Ktm_d.v(Ktm_d.ap()[tk, :], cid), lsem[d_][b][3])
                    fw.dma(gb_l[d_][b][:], GB_d.v(GB_d.ap()[tk, :], cid), lsem[d_][b][4])

                for d_ in range(2):
                    loads(0, d_)
                for s in range(NT):
                    for d_ in range(2):
                        if s + 1 < NT:
                            loads(s + 1, d_)
                        cid = order[d_][s]
                        b = s % NB
                        qT, kT, vt, kt, gb = qT_s[d_][b], kT_s[d_][b], v_s[d_][b], k_s[d_][b], gb_l[d_][b]
                        g4 = gb[:, d_ * 4:d_ * 4 + 4]; be4 = gb[:, 8 + d_ * 4:12 + d_ * 4]
                        last = 127 if d_ == 0 else 0
                        Mk = MASK[d_]; Mks = MASKS[d_]
                        tt(fw, "pool", gU[:], Mk[:].re("p (o i) -> p o i", o=1).bc([128, 4, 128]),
                           g4.re("p (u o) -> p u o", o=1).bc([128, 4, 128]), ALU.mult)
                        mm(fw, psA[:], ones_f[:], gU[:].re("p u i -> p (u i)"))
                        mm(fw, psS[:, 0:4], Mk[:], g4)
                        pA3 = psA[:].re("p (u i) -> p u i", u=4)
                        ts(fw, "dve", ngc[:], psS[:, 0:4], -1.0, None, ALU.mult)
                        act(fw, Gtm[:], psS[:, 0:4], AF.Exp)
                        ts(fw, "dve", nG[:], Gtm[:], -1.0, None, ALU.mult)
                        tt(fw, "dve", rem[:], pA3[:, :, last], ngc[:], ALU.add)
                        act(fw, krem[:], rem[:], AF.Exp)
                        act(fw, tot[:], pA3[:, :, last], AF.Exp)
                        act(fw, Gbc[:].re("p u i -> p (u i)"), psA[:], AF.Exp)
                        for u in range(4):
                            act(fw, E[:, u, :], pA3[:, u, :], AF.Exp, bias=ngc[:, u:u + 1])
                        stt(fw, "dve", DTm[:], E[:], 1.0, Mk[:].re("p (o i) -> p o i", o=1).bc([128, 4, 128]), ALU.min, ALU.mult)
                        stt(fw, "dve", DTs[:], E[:], 1.0, Mks[:].re("p (o i) -> p o i", o=1).bc([128, 4, 128]), ALU.min, ALU.mult)
                        tt(fw, "pool", qdT[:], qT[:], Gbc[:], ALU.mult)
                        tt(fw, "pool", KD[:], kt[:], krem[:].re("p (u o) -> p u o", o=1).bc([128, 4, 128]), ALU.mult)
                        pk, pq_ = psK[0], psK[1]
                        for u in range(4):
                            mm(fw, pk[:, u * 128:(u + 1) * 128], kT[:, u, :], kT[:, u, :])
                            mm(fw, pq_[:, u * 128:(u + 1) * 128], kT[:, u, :], qT[:, u, :])
                        pk3 = pk[:].re("p (u i) -> p u i", u=4); pq3 = pq_[:].re("p (u i) -> p u i", u=4)
                        for u in range(4):
                            stt(fw, "dve", Af[:, u, :], pk3[:, u, :], be4[:, u:u + 1], DTs[:, u, :], ALU.mult, ALU.mult)
                        tt(fw, "dve", QKT[:], pq3, DTm[:], ALU.mult)
                        pi = psI[0]
                        for u in range(4):
                            tr(fw, pi[:, u * 128:(u + 1) * 128], Af[:, u, :], ident_f[:])
                        cp(fw, "act", ATf[:].re("p u i -> p (u i)"), pi[:])
                        tt(fw, "pool", P[0][:], identb4, Af[:], ALU.subtract)
                        Xc, XTc = Af, ATf
                        pcur = 0
                        for lvl in range(1, 7):
                            nx, nxt_ = X[lvl % 2], XT[lvl % 2]
                            lastl = lvl == 6
                            pa, pb = psI[0], psI[1]
                            for u in range(4):
                                mm(fw, pb[:, u * 128:(u + 1) * 128], Xc[:, u, :], XTc[:, u, :])
                            cp(fw, "act", nxt_[:].re("p u i -> p (u i)"), pb[:])
                            if not lastl:
                                for u in range(4):
                                    mm(fw, pa[:, u * 128:(u + 1) * 128], XTc[:, u, :], Xc[:, u, :])
                                cp(fw, "dve", nx[:].re("p u i -> p (u i)"), pa[:])
                            pc_ = psI[0] if lastl else psC[0]
                            for u in range(4):
                                mm(fw, pc_[:, u * 128:(u + 1) * 128], nxt_[:, u, :], P[pcur][:, u, :])
                            if lastl:
                                tt(fw, "dve", P16[:].re("p u i -> p (u i)"), pc_[:], P[pcur][:].re("p u i -> p (u i)"), ALU.add)
                            else:
                                tt(fw, "pool" if False else "dve", P[1 - pcur][:].re("p u i -> p (u i)"), pc_[:], P[pcur][:].re("p u i -> p (u i)"), ALU.add)
                                pcur = 1 - pcur
                            Xc, XTc = nx, nxt_
                        Sf, Sb = S32[d_], S16[d_]
                        pc = psC[1]
                        for u in range(4):
                            mm(fw, pc[:, u * 128:(u + 1) * 128], kT[:, u, :], Sb[:, u, :])
                        pc3 = pc[:].re("p (u i) -> p u i", u=4)
                        for u in range(4):
                            stt(fw, "dve", Rr[:, u, :], pc3[:, u, :], nG[:, u:u + 1], vt[:, u, :], ALU.mult, ALU.add)
                        pv = psC[0]
                        for u in range(4):
                            mm(fw, pv[:, u * 128:(u + 1) * 128], P16[:, u, :], Rr[:, u, :])
                        pv3 = pv[:].re("p (u i) -> p u i", u=4)
                        tt(fw, "dve", VN[:], pv3, be4.re("p (u o) -> p u o", o=1).bc([128, 4, 128]), ALU.mult)
                        if cid >= 2:
                            po = psK[0]
                            for u in range(4):
                                mm(fw, po[:, u * 128:(u + 1) * 128], qdT[:, u, :], Sb[:, u, :], True, False)
                                mm(fw, po[:, u * 128:(u + 1) * 128], QKT[:, u, :], VN[:, u, :], False, True)
                            ob = Osb[s % 2]
                            cp(fw, "act", ob[:].re("p u i -> p (u i)"), po[:])
                            fw.dma(OA_d[d_].v(OA_d[d_].ap()[cid * 128:(cid + 1) * 128, :], cid), ob[:].re("p u i -> p (u i)"), osem[s % 2])
                        pu = psK[1]
                        for u in range(4):
                            mm(fw, pu[:, u * 128:(u + 1) * 128], KD[:, u, :], VN[:, u, :])
                        pu3 = pu[:].re("p (u i) -> p u i", u=4)
                        for u in range(4):
                            stt(fw, "dve", Sf[:, u, :], Sf[:, u, :], tot[:, u:u + 1], pu3[:, u, :], ALU.mult, ALU.add)
                        cp(fw, "act", Sb[:], Sf[:])
                dbg_out("S32f", S32[0][:], [128, 4, 128])
                dbg_out("S32b", S32[1][:], [128, 4, 128])
                fw.phase_end()
                dbg_dram("OAf", OA_d[0], [NTOK, 512], F32)
                dbg_dram("OAb", OA_d[1], [NTOK, 512], F32)
            fw.stack = st


        def xsrc(tk):
            if tk < 2:
                return xa.v(xa.ap()[tk * 128:(tk + 1) * 128, :])
            return xa.v(xa.ap()[256:, :].rearrange("(r w) d -> w r d", w=W)[tk - 2])

        TOT_s = [fw.sb([128, 4, NT], F32) for _ in range(2)]
        if "bprep" in stages or "all" in stages:
            with ExitStack() as ph:
                fw.stack = ph
                WQF = fw.sb([128, 8, 1536], BF16); WV = fw.sb([128, 8, 512], BF16); WG = fw.sb([128, 8, 512], BF16)
                kpn = lambda dr: dr.ap().rearrange("(k p) n -> p k n", p=128)
                fw.dma(WQF[:, :, 0:512], wBq.v(kpn(wBq)), fw.dsem(), queue="pool")
                fw.dma(WQF[:, :, 512:1536], wBf.v(kpn(wBf)), fw.dsem(), queue="pool")
                fw.dma(WV[:], wBv.v(kpn(wBv)), fw.dsem(), queue="pool")
                fw.dma(WG[:], wBg.v(kpn(wBg)), fw.dsem(), queue="pool")
                xts = [fw.sb([128, D], F32) for _ in range(2)]
                for x_ in xts:
                    xt_sem[id(x_)] = fw.dsem()
                junk = fw.sb([128, D], BF16); ssq = fw.sb([128, 1], F32); rstd = fw.sb([128, 1], F32)
                xn = fw.sb([128, D], BF16)
                pst = [fw.ps([128, D], BF16) for _ in range(2)]
                hT = [fw.sb([128, 8, 512], BF16) for _ in range(2)]
                psq = [fw.ps([128, 512]) for _ in range(2)]
                psg = [fw.ps([128, 512]) for _ in range(2)]
                pstm = fw.ps([128, 512], BF16)
                lb = fw.sb([128, 4], F32); oml = fw.sb([128, 4], F32)
                tt(fw, "dve", lb[:], lbl_s[:, 0:4], lbl_s[:, 4:8], ALU.subtract)
                act(fw, lb[:], lb[:], AF.Sigmoid)
                ts(fw, "dve", oml[:], lb[:], -1.0, 1.0, ALU.mult, ALU.add)
                Qs = fw.sb([128, 4, 512], F32)
                LF = [fw.sb([128, 4, 512], F32) for _ in range(2)]
                KK_ = [fw.sb([128, 4, 512], F32) for _ in range(2)]
                onesb = fw.sb([128, 4, 512], F32)
                memset(fw, "pool", onesb[:], 1.0)
                CUM = fw.sb([128, 4, 512], F32); BC = fw.sb([128, 4, 512], F32); D1 = fw.sb([128, 4, 512], F32)
                EX = fw.sb([128, 4, 512], F32)
                OFF = fw.sb([128, 4, 4], F32); TT_ = fw.sb([128, 4, 4], F32)
                OUT16 = [fw.sb([128, 4, 512], BF16) for _ in range(4)]
                v_s = fw.sb([128, 512], BF16); g_s = fw.sb([128, 512], BF16); kd_s = fw.sb([128, 512], BF16)
                sem_b = {k_: fw.dsem() for k_ in ("v", "g", "qe", "ke", "qd", "kd")}
                groups = [(0, 0, 2)] + [(1, 2 + 4 * g, 4) for g in range(W // 4)]
                tcount = 0; qc = 0
                for gi, (seq, t0, ntl) in enumerate(groups):
                    n = 128 * ntl
                    h_ = hT[gi % 2]
                    for tl in range(ntl):
                        tk = t0 + tl
                        xt = xts[tcount % 2]; ps_ = pst[tcount % 2]; tcount += 1
                        w = 1 if seq == 0 else 0
                        make_hT(xsrc(tk), h_[:, :, tl * 128:(tl + 1) * 128], w, A1[:, :, w], modT[:, 0:8, w], xt, junk, ssq, rstd, xn, ps_)
                        pg = psg[tl % 2]
                        for k_ in range(8):
                            mm(fw, pg[:], h_[:, k_, tl * 128:(tl + 1) * 128], WV[:, k_, :], k_ == 0, k_ == 7)
                        cp(fw, "act", v_s[:], pg[:])
                        fw.dma(VB_d.v(VB_d.ap()[tk * 128:(tk + 1) * 128, :], tk), v_s[:], sem_b["v"])
                        pg = psg[(tl + 1) % 2]
                        for k_ in range(8):
                            mm(fw, pg[:], h_[:, k_, tl * 128:(tl + 1) * 128], WG[:, k_, :], k_ == 0, k_ == 7)
                        act(fw, g_s[:], pg[:], AF.Silu)
                        fw.dma(GBG_d.v(GBG_d.ap()[tk * 128:(tk + 1) * 128, :], tk), g_s[:], sem_b["g"])
                    for m in range(12):
                        pq = psq[qc % 2]; qc += 1
                        for k_ in range(8):
                            mm(fw, pq[:, 0:n], WQF[:, k_, m * 128:(m + 1) * 128], h_[:, k_, 0:n], k_ == 0, k_ == 7)
                        hh = m % 4
                        if m < 4:
                            act(fw, Qs[:, hh, 0:n], pq[:, 0:n], AF.Silu)
                        else:
                            d_ = (m - 4) // 4
                            act(fw, LF[d_][:, hh, 0:n], pq[:, 0:n], AF.Sigmoid)
                            ts(fw, "dve", LF[d_][:, hh, 0:n], LF[d_][:, hh, 0:n], oml[:, hh:hh + 1], lb[:, hh:hh + 1], ALU.mult, ALU.add)
                            ts(fw, "pool", KK_[d_][:, hh, 0:n], LF[d_][:, hh, 0:n], -1.0, 1.0, ALU.mult, ALU.add)
                    for d_ in range(2):
                        act(fw, LF[d_][:, :, 0:n], LF[d_][:, :, 0:n], AF.Ln)
                        lf = LF[d_]
                        if n == 512:
                            segs = [(CUM[:].re("p h i -> p (h i)"), onesb[:].re("p h i -> p (h i)"), lf[:].re("p h i -> p (h i)"))]
                        else:
                            segs = [(CUM[:, hh, 0:n], onesb[:, hh, 0:n], lf[:, hh, 0:n]) for hh in range(4)]
                        for (o1, a0, a1) in segs:
                            fw.op("dve", lambda e, o1=o1.ap, a0=a0.ap, a1=a1.ap: e.tensor_tensor_scan(out=o1, data0=a0, data1=a1, initial=0.0, op0=ALU.mult, op1=ALU.add),
                                  [onesb, lf], [CUM])
                        c4 = CUM[:, :, 0:n].re("p h (t i) -> p h t i", i=128)
                        l4 = lf[:, :, 0:n].re("p h (t i) -> p h t i", i=128)
                        b4 = BC[:, :, 0:n].re("p h (t i) -> p h t i", i=128)
                        d4 = D1[:, :, 0:n].re("p h (t i) -> p h t i", i=128)
                        tt(fw, "dve", OFF[:, :, 0:ntl], c4[:, :, :, 0], l4[:, :, :, 0], ALU.subtract)
                        offb = OFF[:, :, 0:ntl].re("p h (t o) -> p h t o", o=1).bc([128, 4, ntl, 128])
                        tt(fw, "dve", b4, c4, offb, ALU.subtract)
                        if d_ == 1:
                            tt(fw, "dve", TT_[:, :, 0:ntl], b4[:, :, :, 127], b4[:, :, :, 127], ALU.max)
                            ttb = TT_[:, :, 0:ntl].re("p h (t o) -> p h t o", o=1).bc([128, 4, ntl, 128])
                            tt(fw, "dve", b4, ttb, b4, ALU.subtract)
                            tt(fw, "dve", b4, b4, l4, ALU.add)
                        lastp = 127 if d_ == 0 else 0
                        act(fw, TOT_s[d_][:, :, t0:t0 + ntl], b4[:, :, :, lastp], AF.Exp)
                        q3 = Qs[:, :, 0:n]; k3 = KK_[d_][:, :, 0:n]
                        sc_q = 128.0 ** -0.5
                        act(fw, EX[:, :, 0:n], BC[:, :, 0:n], AF.Exp)
                        stt(fw, "dve", OUT16[2][:, :, 0:n], q3, sc_q, EX[:, :, 0:n], ALU.mult, ALU.mult)
                        refb = b4[:, :, :, 64:65].bc([128, 4, ntl, 128])
                        tt(fw, "pool", d4, b4, refb, ALU.subtract)
                        act(fw, EX[:, :, 0:n], D1[:, :, 0:n], AF.Exp)
                        stt(fw, "dve", OUT16[0][:, :, 0:n], q3, sc_q, EX[:, :, 0:n], ALU.mult, ALU.mult)
                        act(fw, EX[:, :, 0:n], D1[:, :, 0:n], AF.Exp, scale=-1.0)
                        tt(fw, "pool", OUT16[1][:, :, 0:n], k3, EX[:, :, 0:n], ALU.mult)
                        lastb = b4[:, :, :, lastp:lastp + 1].bc([128, 4, ntl, 128])
                        tt(fw, "pool", d4, b4, lastb, ALU.subtract)
                        act(fw, EX[:, :, 0:n], D1[:, :, 0:n], AF.Exp, scale=-1.0)
                        tt(fw, "pool", OUT16[3][:, :, 0:n], k3, EX[:, :, 0:n], ALU.mult)
                        tok0 = 128 * t0
                        for (dr, src, key) in ((QE_d[d_], OUT16[0], "qe"), (KE_d[d_], OUT16[1], "ke"), (QD_d[d_], OUT16[2], "qd")):
                            fw.dma(MV(dr.ap().rearrange("h d t -> d h t")[:, :, tok0:tok0 + n], dr.bufs[t0:t0 + ntl]),
                                   src[:, :, 0:n], sem_b[key], writes=[V(None, b) for b in dr.bufs[t0:t0 + ntl]])
                        for tl in range(ntl):
                            for hh in range(4):
                                tr(fw, pstm[:, hh * 128:(hh + 1) * 128], OUT16[3][:, hh, tl * 128:(tl + 1) * 128], ident_bf[:])
                            cp(fw, "act", kd_s[:], pstm[:])
                            tk = t0 + tl
                            fw.dma(KDtm_d[d_].v(KDtm_d[d_].ap()[tk * 128:(tk + 1) * 128, :], tk), kd_s[:], sem_b["kd"])
                fw.phase_end()
                for d_ in range(2):
                    dbg_dram(f"QE{d_}", QE_d[d_], [4, 128, NTOK], BF16); dbg_dram(f"KE{d_}", KE_d[d_], [4, 128, NTOK], BF16)
                    dbg_dram(f"QD{d_}", QD_d[d_], [4, 128, NTOK], BF16); dbg_dram(f"KD{d_}", KDtm_d[d_], [NTOK, 512], BF16)
                    dbg_out(f"TOT{d_}", TOT_s[d_][:], [128, 4, NT])
                dbg_dram("VB", VB_d, [NTOK, 512], BF16)
            fw.stack = st

        if "bchain" in stages or "all" in stages:
            with ExitStack() as ph:
                fw.stack = ph
                order = [list(range(NT)), [1, 0] + list(range(NT - 1, 1, -1))]
                S32 = [fw.sb([128, 4, 128], F32) for _ in range(2)]
                S16 = [fw.sb([128, 4, 128], BF16) for _ in range(2)]
                for d_ in range(2):
                    memset(fw, "pool", S32[d_][:], 0.0)
                    memset(fw, "pool", S16[d_][:], 0.0)
                NB = 2
                mk = lambda dt=BF16: [[fw.sb([128, 4, 128], dt) for _ in range(NB)] for _ in range(2)]
                qe_s, ke_s, qd_s, kd_s2, vb_s = mk(), mk(), mk(), mk(), mk()
                lsem = [[[fw.dsem() for _ in range(5)] for _ in range(NB)] for _ in range(2)]
                psSC = [fw.ps([128, 512]) for _ in range(2)]
                psO = [fw.ps([128, 512]) for _ in range(2)]
                psU = [fw.ps([128, 512]) for _ in range(2)]
                SC16 = [fw.sb([128, 4, 128], BF16) for _ in range(2)]
                Osb = [fw.sb([128, 512], F32) for _ in range(2)]
                osem = [fw.dsem() for _ in range(2)]

                def loadsb(s, d_):
                    cid = order[d_][s]
                    b = s % NB
                    tk = slice(cid * 128, (cid + 1) * 128)
                    hd = lambda dr: dr.v(dr.ap().rearrange("h d t -> d h t")[:, :, tk], cid)
                    fw.dma(qe_s[d_][b][:], hd(QE_d[d_]), lsem[d_][b][0])
                    fw.dma(ke_s[d_][b][:], hd(KE_d[d_]), lsem[d_][b][1])
                    fw.dma(qd_s[d_][b][:], hd(QD_d[d_]), lsem[d_][b][2])
                    fw.dma(kd_s2[d_][b][:].re("p h d -> p (h d)"), KDtm_d[d_].v(KDtm_d[d_].ap()[tk, :], cid), lsem[d_][b][3])
                    fw.dma(vb_s[d_][b][:].re("p h d -> p (h d)"), VB_d.v(VB_d.ap()[tk, :], cid), lsem[d_][b][4])

                for d_ in range(2):
                    loadsb(0, d_)
                def bstep(s, d_):
                    if s + 1 < NT:
                        loadsb(s + 1, d_)
                    cid = order[d_][s]
                    b = s % NB
                    qe, ke, qd, kd, vb = qe_s[d_][b], ke_s[d_][b], qd_s[d_][b], kd_s2[d_][b], vb_s[d_][b]
                    Sf, Sb = S32[d_], S16[d_]
                    if cid >= 2:
                        psc = psSC[d_]
                        for u in range(4):
                            mm(fw, psc[:, u * 128:(u + 1) * 128], ke[:, u, :], qe[:, u, :])
                        yield
                        sc16 = SC16[d_]
                        tt(fw, "dve", sc16[:], psc[:].re("p (u i) -> p u i", u=4),
                           MASK[d_][:].re("p (o i) -> p o i", o=1).bc([128, 4, 128]), ALU.mult)
                        yield
                        po = psO[d_]
                        for u in range(4):
                            mm(fw, po[:, u * 128:(u + 1) * 128], qd[:, u, :], Sb[:, u, :], True, False)
                            mm(fw, po[:, u * 128:(u + 1) * 128], sc16[:, u, :], vb[:, u, :], False, True)
                        yield
                        ob = Osb[d_]
                        cp(fw, "act", ob[:], po[:])
                        fw.dma(OB_d[d_].v(OB_d[d_].ap()[cid * 128:(cid + 1) * 128, :], cid), ob[:], osem[d_])
                        yield
                    pu = psU[d_]
                    for u in range(4):
                        mm(fw, pu[:, u * 128:(u + 1) * 128], kd[:, u, :], vb[:, u, :])
                    yield
                    totb = TOT_s[d_][:, :, cid:cid + 1].bc([128, 4, 128])
                    tt(fw, "pool", Sf[:], Sf[:], totb, ALU.mult)
                    yield
                    tt(fw, "dve", Sf[:], Sf[:], pu[:].re("p (u i) -> p u i", u=4), ALU.add)
                    yield
                    cp(fw, "act", Sb[:], Sf[:])
                    yield

                for s in range(NT):
                    gens = [bstep(s, 0), bstep(s, 1)]
                    while gens:
                        for g_ in list(gens):
                            try:
                                next(g_)
                            except StopIteration:
                                gens.remove(g_)
                dbg_out("SBf", S32[0][:], [128, 4, 128])
                dbg_out("SBb", S32[1][:], [128, 4, 128])
                fw.phase_end()
                dbg_dram("OBf", OB_d[0], [NTOK, 512], F32)
                dbg_dram("OBb", OB_d[1], [NTOK, 512], F32)
            fw.stack = st

        if "fin" in stages or "all" in stages:
            with ExitStack() as ph:
                fw.stack = ph
                of_s = [fw.sb([128, 4, 128], F32) for _ in range(2)]; ob_s = [fw.sb([128, 4, 128], F32) for _ in range(2)]
                gt_s = [fw.sb([128, 4, 128], BF16) for _ in range(2)]
                fsem = [[fw.dsem() for _ in range(3)] for _ in range(2)]
                sq = fw.sb([128, 4, 128], F32); ssq4 = fw.sb([128, 4], F32); on = fw.sb([128, 4, 128], F32)
                y16 = [fw.sb([128, 512], BF16) for _ in range(2)]
                ysem = [fw.dsem() for _ in range(2)]
                cnt = 0
                for mixer in range(2):
                    Od = OA_d if mixer == 0 else OB_d
                    Gd = GA_d if mixer == 0 else GBG_d
                    nw = gnw_s if mixer == 0 else hnw_s
                    for c in range(W):
                        cid = 2 + c
                        b = cnt % 2; cnt += 1
                        tk = slice(cid * 128, (cid + 1) * 128)
                        fw.dma(of_s[b][:].re("p h d -> p (h d)"), Od[0].v(Od[0].ap()[tk, :], cid), fsem[b][0])
                        fw.dma(ob_s[b][:].re("p h d -> p (h d)"), Od[1].v(Od[1].ap()[tk, :], cid), fsem[b][1])
                        fw.dma(gt_s[b][:].re("p h d -> p (h d)"), Gd.v(Gd.ap()[tk, :], cid), fsem[b][2])
                        tt(fw, "pool", of_s[b][:], of_s[b][:], ob_s[b][:], ALU.add)
                        tt(fw, "pool", sq[:], of_s[b][:], of_s[b][:], ALU.mult)
                        fw.op("dve", lambda e, o=ssq4.t[:], i=sq.t[:]: e.tensor_reduce(out=o, in_=i, axis=AX.X, op=ALU.add), [sq], [ssq4])
                        act(fw, ssq4[:], ssq4[:], AF.Sqrt, bias=eps_col[:], scale=1.0 / 128)
                        recip(fw, ssq4[:], ssq4[:])
                        tt(fw, "dve", on[:], of_s[b][:], ssq4[:].re("p (h o) -> p h o", o=1).bc([128, 4, 128]), ALU.mult)
                        tt(fw, "pool", on[:], on[:], nw[:].re("p (o d) -> p o d", o=1).bc([128, 4, 128]), ALU.mult)
                        yb = y16[b]
                        tt(fw, "dve", yb[:].re("p (h d) -> p h d", h=4), on[:], gt_s[b][:], ALU.mult)
                        if mixer == 0:
                            dst = XCH_d.ap()[c * 128:(c + 1) * 128, 0:512]
                        else:
                            dst = XCH_d.ap().rearrange("(r w) d -> w r d", w=W)[c][:, 512:1024]
                        fw.dma(XCH_d.v(dst, mixer * W + c), yb[:], ysem[b])
                RB = min(512, S)
                for j_ in range(S // RB):
                    src_ap = XCH_d.ap()[j_ * RB:(j_ + 1) * RB, :].opt(); dst_ap = XG_d.ap()[j_ * 2 * RB:(j_ + 1) * 2 * RB, :].opt()
                    fw.op("pool", lambda e, src_ap=src_ap, dst_ap=dst_ap: e.collective_compute("AllGather", ALU.bypass, replica_groups=[[0, 1], [2, 3], [4, 5], [6, 7]],
                                                                 ins=[src_ap], outs=[dst_ap]), [V(None, b_) for b_ in XCH_d.bufs], [V(None, XG_d.bufs[0])])
                b_src = BAR_s.ap().opt(); b_dst = BAR_g.ap().opt()
                fw.op("pool", lambda e: e.collective_compute("AllGather", ALU.bypass, replica_groups=[[0, 1], [2, 3], [4, 5], [6, 7]],
                                                             ins=[b_src], outs=[b_dst]), [V(None, XG_d.bufs[0])], [V(None, XG_d.bufs[0])])
                fw.phase_end()
                dbg_dram("XG", XG_d, [2 * S, 1024], BF16)
            fw.stack = st

        NTL = SH // 128
        if "merge" in stages or "all" in stages:
            with ExitStack() as ph:
                fw.stack = ph
                kpn = lambda dr: dr.ap().rearrange("(k p) n -> p k n", p=128)
                Wmg = fw.sb([128, 8, 2048], BF16); Wpa = fw.sb([128, 8, D], BF16); Wpb = fw.sb([128, 8, D], BF16); Wo = fw.sb([128, 8, D], BF16)
                for wt_, dr_ in ((Wmg, wMG), (Wpa, w_pa), (Wpb, w_pb), (Wo, w_o)):
                    fw.dma(wt_[:], dr_.v(kpn(dr_)), fw.dsem(), queue="pool")
                xts = [fw.sb([128, D], F32) for _ in range(2)]
                for x_ in xts:
                    xt_sem[id(x_)] = fw.dsem()
                junk = fw.sb([128, D], BF16); ssq = fw.sb([128, 1], F32); rstd = fw.sb([128, 1], F32)
                xn = fw.sb([128, D], BF16)
                pst = [fw.ps([128, D], BF16) for _ in range(2)]
                psX = fw.ps([128, 2048])
                hT = fw.sb([128, 8, 128], BF16)
                gates = fw.sb([128, 2048], F32)
                oab = [fw.sb([128, 2, D], BF16) for _ in range(2)]
                osem = [[fw.dsem() for _ in range(4)] for _ in range(2)]
                oT = fw.sb([128, 2, 8, 128], BF16)
                Lx = [fw.sb([128, 2, 2, D], BF16) for _ in range(2)]
                selt = fw.sb([128, D], BF16)
                hsel_s = load_small(hsel, [128, 2])
                m1 = fw.sb([128, D], F32); m16 = fw.sb([128, D], BF16); mT = fw.sb([128, 8, 128], BF16)
                x1 = [fw.sb([128, D], F32) for _ in range(2)]; x1sem = [fw.dsem() for _ in range(2)]
                h2T = [fw.sb([128, 8, 128], BF16) for _ in range(2)]; h2sem = [fw.dsem() for _ in range(2)]
                rank_of_half = [0, 1]
                for tl in range(NTL):
                    b = tl % 2
                    xt = xts[b]
                    t0 = tl * 128
                    make_hT(xh.v(xh.ap()[t0:t0 + 128, :]), hT[:], 0, A1[:, :, 0], modT[:, 0:8, 0], xt, junk, ssq, rstd, xn, pst[b])
                    for r_ in range(2):
                        for hh in range(2):
                            tok_ = hh * SH + t0
                            RB = min(512, S)
                            row0 = (tok_ // RB) * 2 * RB + r_ * RB + tok_ % RB
                            fw.dma(Lx[b][:, r_, hh, :], XG_d.v(XG_d.ap()[row0:row0 + 128, :]), osem[b][r_ * 2 + hh])
                        ts(fw, "dve", selt[:], Lx[b][:, r_, 0, :], hsel_s[:, 0:1], None, ALU.mult)
                        stt(fw, "dve", selt[:], Lx[b][:, r_, 1, :], hsel_s[:, 1:2], selt[:], ALU.mult, ALU.add)
                        cp(fw, "pool", oab[b][:, :, r_ * 512:(r_ + 1) * 512], selt[:].re("p (m c) -> p m c", m=2))
                    for nb in range(4):
                        for k_ in range(8):
                            mm(fw, psX[:, nb * 512:(nb + 1) * 512], hT[:, k_, :], Wmg[:, k_, nb * 512:(nb + 1) * 512], k_ == 0, k_ == 7)
                    act(fw, gates[:], psX[:], AF.Sigmoid)
                    for mx in range(2):
                        pt = pst[(b + 1 + mx) % 2]
                        for k_ in range(8):
                            tr(fw, pt[:, k_ * 128:(k_ + 1) * 128], oab[b][:, mx, k_ * 128:(k_ + 1) * 128], ident_bf[:])
                        cp(fw, "act" if mx else "dve", oT[:, mx, :, :].re("p k t -> p (k t)"), pt[:])
                    for mx, Wp in ((0, Wpa), (1, Wpb)):
                        for nb in range(2):
                            for k_ in range(8):
                                mm(fw, psX[:, mx * 1024 + nb * 512: mx * 1024 + (nb + 1) * 512], oT[:, mx, k_, :], Wp[:, k_, nb * 512:(nb + 1) * 512], k_ == 0, k_ == 7)
                    tt(fw, "dve", m1[:], psX[:, 0:1024], gates[:, 0:1024], ALU.mult)
                    tt(fw, "dve", gates[:, 1024:2048], psX[:, 1024:2048], gates[:, 1024:2048], ALU.mult)
                    tt(fw, "pool", m16[:], m1[:], gates[:, 1024:2048], ALU.add)
                    pt = pst[b]
                    for k_ in range(8):
                        tr(fw, pt[:, k_ * 128:(k_ + 1) * 128], m16[:, k_ * 128:(k_ + 1) * 128], ident_bf[:])
                    cp(fw, "act", mT[:].re("p k t -> p (k t)"), pt[:])
                    for nb in range(2):
                        for k_ in range(8):
                            mm(fw, psX[:, nb * 512:(nb + 1) * 512], mT[:, k_, :], Wo[:, k_, nb * 512:(nb + 1) * 512], k_ == 0, k_ == 7)
                    tt(fw, "dve", m1[:], psX[:, 0:1024], g1_bc[:], ALU.mult)
                    tt(fw, "pool", x1[b][:], m1[:], xt[:], ALU.add)
                    fw.dma(X1_d.v(X1_d.ap()[t0:t0 + 128, :], tl), x1[b][:], x1sem[b])
                    if tl == 0:
                        dbg_out("d_x1", x1[b][:], [128, D]); dbg_out("d_m1", m1[:], [128, D]); dbg_out("d_gates", gates[:], [128, 2048])
                        dbg_out("d_oab", oab[b][:], [128, 2, D], BF16); dbg_out("d_Lx", Lx[b][:], [128, 2, 2, D], BF16); dbg_out("d_hsel", hsel_s[:], [128, 2]); dbg_out("d_hT", hT[:], [128, 8, 128], BF16); dbg_out("d_m16", m16[:], [128, D], BF16)
                    act(fw, junk[:], x1[b][:], AF.Square, accum=ssq[:])
                    act(fw, rstd[:], ssq[:], AF.Sqrt, bias=eps_col[:], scale=1.0 / D)
                    recip(fw, rstd[:], rstd[:])
                    ts(fw, "dve", xn[:], x1[b][:], rstd[:], None, ALU.mult)
                    pt = pst[(b + 1) % 2]
                    for k_ in range(8):
                        tr(fw, pt[:, k_ * 128:(k_ + 1) * 128], xn[:, k_ * 128:(k_ + 1) * 128], ident_bf[:])
                    pv = pt[:].re("p (k t) -> p k t", k=8)
                    tt(fw, "dve", h2T[b][:], pv, A2[:].re("p (k o) -> p k o", o=1).bc([128, 8, 128]), ALU.mult)
                    tt(fw, "pool", h2T[b][:], h2T[b][:], modT[:, 24:32, 0].re("p (k o) -> p k o", o=1).bc([128, 8, 128]), ALU.add)
                    fw.dma(H2T_d.v(H2T_d.ap()[tl].rearrange("p (k t) -> p k t", k=8), tl), h2T[b][:], h2sem[b])
                dbg_out("gates", gates[:], [128, 2048]); dbg_out("oab", oab[(NTL - 1) % 2][:], [128, 2, D], BF16)
                dbg_out("m1", m1[:], [128, D]); dbg_out("x1s", x1[(NTL - 1) % 2][:], [128, D]); dbg_out("xts", xts[(NTL - 1) % 2][:], [128, D]); dbg_out("hTm", hT[:], [128, 8, 128], BF16); dbg_out("Lx", Lx[(NTL - 1) % 2][:], [128, 2, 2, D], BF16)
                fw.phase_end()
                dbg_dram("x1", X1_d, [SH, D], F32)
            fw.stack = st

        if "peer" in stages or "all" in stages:
            with ExitStack() as ph:
                fw.stack = ph
                kpn = lambda dr: dr.ap().rearrange("(k p) n -> p k n", p=128)
                Wq = fw.sb([128, 8, 2048], BF16); skT_s = fw.sb([128, 16, 128], BF16)
                fw.dma(Wq[:], w_q.v(kpn(w_q)), fw.dsem(), queue="pool")
                fw.dma(skT_s[:].re("p a b -> p (a b)"), skT.v(skT.ap()), fw.dsem(), queue="pool")
                h2 = [fw.sb([128, 8, 128], BF16) for _ in range(2)]; h2s = [fw.dsem() for _ in range(2)]
                pqy = [fw.ps([128, 128]) for _ in range(2)]
                pss = fw.ps([128, 2048])
                qT = [fw.sb([128, 128], BF16) for _ in range(2)]
                s_sb = [fw.sb([128, 16, 128], F32) for _ in range(2)]; ssem = [fw.dsem() for _ in range(2)]
                T16 = fw.sb([128, 16, 16], F32); W1 = fw.sb([128, 128], F32)
                cand = fw.sb([128, 8, 256], F32); W2 = fw.sb([128, 256], F32); W3 = fw.sb([128, 256], F32)
                C24 = fw.sb([128, 8, 24], F32)
                stt_s = [fw.sb([128, 16], F32) for _ in range(2)]; stsem = [fw.dsem() for _ in range(2)]
                e16 = fw.sb([128, 8, 16], F32); Zs = fw.sb([128, 8], F32)

                def dve(f, reads, writes):
                    return fw.op("dve", f, reads, writes)
                for tl in range(NTL):
                    b = tl % 2
                    fw.dma(h2[b][:], H2T_d.v(H2T_d.ap()[tl].rearrange("p (k t) -> p k t", k=8), tl), h2s[b])
                    for hc in range(16):
                        pq = pqy[hc % 2]
                        for k_ in range(8):
                            mm(fw, pq[:], Wq[:, k_, hc * 128:(hc + 1) * 128], h2[b][:, k_, :], k_ == 0, k_ == 7)
                        cp(fw, "act" if hc % 2 else "dve", qT[hc % 2][:], pq[:])
                        mm(fw, pss[:, hc * 128:(hc + 1) * 128], qT[hc % 2][:], skT_s[:, hc, :])
                    sb_ = s_sb[b]
                    cp(fw, "act", sb_[:].re("p a b -> p (a b)"), pss[:])
                    fw.dma(SS_d.v(SS_d.ap()[tl * 128:(tl + 1) * 128, :], tl), sb_[:].re("p a b -> p (a b)"), ssem[b])
                    for hc in range(16):
                        dve(lambda e, o=T16.t[:, hc, 0:8], i=sb_.t[:, hc, :]: e.max(out=o, in_=i), [sb_], [T16])
                        dve(lambda e, o=W1.t[:], r=T16.t[:, hc, 0:8], i=sb_.t[:, hc, :]: e.match_replace(out=o, in_to_replace=r, in_values=i, imm_value=-1e30), [sb_, T16], [W1])
                        dve(lambda e, o=T16.t[:, hc, 8:16], i=W1.t[:]: e.max(out=o, in_=i), [W1], [T16])
                    t4 = T16[:].re("p (h c) a -> p h c a", c=2)
                    in0 = t4[:, :, 0, :].re("p h (a o) -> p h a o", o=1).bc([128, 8, 16, 16])
                    in1 = t4[:, :, 1, :].re("p h (o a) -> p h o a", o=1).bc([128, 8, 16, 16])
                    tt(fw, "pool", cand[:].re("p h (a c) -> p h a c", c=16), in0, in1, ALU.add)
                    for h in range(8):
                        dve(lambda e, o=C24.t[:, h, 0:8], i=cand.t[:, h, :]: e.max(out=o, in_=i), [cand], [C24])
                        dve(lambda e, o=W2.t[:], r=C24.t[:, h, 0:8], i=cand.t[:, h, :]: e.match_replace(out=o, in_to_replace=r, in_values=i, imm_value=-1e30), [cand, C24], [W2])
                        dve(lambda e, o=C24.t[:, h, 8:16], i=W2.t[:]: e.max(out=o, in_=i), [W2], [C24])
                        dve(lambda e, o=W3.t[:], r=C24.t[:, h, 8:16], i=W2.t[:]: e.match_replace(out=o, in_to_replace=r, in_values=i, imm_value=-1e30), [W2, C24], [W3])
                        dve(lambda e, o=C24.t[:, h, 16:24], i=W3.t[:]: e.max(out=o, in_=i), [W3], [C24])
                    sts = stt_s[b]
                    tt(fw, "pool", sts[:, 0:8], C24[:, :, 15], C24[:, :, 16], ALU.add)
                    ts(fw, "pool", sts[:, 0:8], sts[:, 0:8], 0.5, None, ALU.mult)
                    tt(fw, "pool", e16[:], C24[:, :, 0:16], C24[:, :, 0:1].bc([128, 8, 16]), ALU.subtract)
                    act(fw, e16[:], e16[:], AF.Exp)
                    dve(lambda e, o=Zs.t[:], i=e16.t[:]: e.tensor_reduce(out=o, in_=i, axis=AX.X, op=ALU.add), [e16], [Zs])
                    act(fw, Zs[:], Zs[:], AF.Ln)
                    tt(fw, "pool", Zs[:], Zs[:], C24[:, :, 0], ALU.add)
                    ts(fw, "pool", sts[:, 8:16], Zs[:], -1.0, None, ALU.mult)
                    tt(fw, "pool", sts[:, 0:8], sts[:, 0:8], sts[:, 8:16], ALU.add)
                    act(fw, sts[:, 0:8], sts[:, 0:8], AF.Exp)
                    fw.dma(ST_d.v(ST_d.ap()[tl * 128:(tl + 1) * 128, :], tl), sts[:], stsem[b])
                fw.phase_end()
                dbg_dram("SS", SS_d, [SH, 2048], F32); dbg_dram("ST", ST_d, [SH, 16], F32)
            fw.stack = st

        if "peer" in stages or "all" in stages:
            with ExitStack() as ph:
                fw.stack = ph
                GT = min(4, NTL)
                h2 = [fw.sb([128, 8, 128], BF16) for _ in range(GT)]
                ss = [fw.sb([128, 8, 2, 128], F32) for _ in range(GT)]
                s1c = [fw.sb([128, 8, 128], F32) for _ in range(GT)]
                stv = [fw.sb([128, 16], F32) for _ in range(GT)]
                ysb = [fw.sb([128, D], F32) for _ in range(GT)]
                lds = [[fw.dsem() for _ in range(3)] for _ in range(GT)]
                NE = 6
                Eb_ = [fw.sb([128, 4, 128], BF16) for _ in range(NE)]
                Gh_ = [fw.sb([128, 512], BF16) for _ in range(NE)]
                Sm_ = [fw.sb([128, 4, 128], F32) for _ in range(NE)]
                NGS = 5
                GsR = [fw.sb([128, 512], BF16) for _ in range(NGS)]
                itc = 0
                Pacc = [fw.sb([128, 512], BF16) for _ in range(2)]
                UTp = [fw.sb([128, 8, 512], BF16) for _ in range(3)]; Vp = [fw.sb([128, 4, D], BF16) for _ in range(3)]
                usem = [fw.dsem() for _ in range(3)]; vsem = [fw.dsem() for _ in range(3)]
                pG = [fw.ps([128, 512]) for _ in range(2)]
                psc = [fw.ps([128, 512]) for _ in range(2)]
                ptz = [fw.ps([128, 512], BF16) for _ in range(2)]
                pyy = [fw.ps([128, 512]) for _ in range(2)]
                gl = [fw.sb([128, 512], BF16) for _ in range(2)]; Zt = [fw.sb([128, 512], BF16) for _ in range(2)]
                ZT = [fw.sb([128, 4, 128], BF16) for _ in range(2)]
                x1l = fw.sb([128, D], F32); x1ls = fw.dsem()
                tmpf = fw.sb([128, D], F32); junk = fw.sb([128, D], BF16); ssq = fw.sb([128, 1], F32); rstd = fw.sb([128, 1], F32)
                fnw_s = fw.sb([128, D], F32)
                fw.dma(fnw_s[:], fnw.v(fnw.ap()), fw.dsem())
                outs = [fw.sb([128, D], F32) for _ in range(2)]; outsem = [fw.dsem() for _ in range(2)]
                uTv = uT.ap().rearrange("(k p) e -> p k e", p=128)
                evv = ev.ap().rearrange("(c p) d -> p c d", p=128)
                pcnt = 0; zc = 0; ec = 0
                for g0 in range(0, NTL, GT):
                    tiles = list(range(g0, min(g0 + GT, NTL)))
                    for ti, tl in enumerate(tiles):
                        fw.dma(h2[ti][:], H2T_d.v(H2T_d.ap()[tl].rearrange("p (k t) -> p k t", k=8), tl), lds[ti][0])
                        fw.dma(ss[ti][:].re("p h c k -> p (h c k)"), SS_d.v(SS_d.ap()[tl * 128:(tl + 1) * 128, :], tl), lds[ti][1])
                        fw.dma(stv[ti][:], ST_d.v(ST_d.ap()[tl * 128:(tl + 1) * 128, :], tl), lds[ti][2])
                        tt(fw, "pool", s1c[ti][:], ss[ti][:, :, 0, :], stv[ti][:, 8:16].re("p (h o) -> p h o", o=1).bc([128, 8, 128]), ALU.add)
                    def g_body(gslot, gsi, ti, pc):
                        nonlocal ec
                        z = gslot
                        for h in range(8):
                            Eb = Eb_[ec % NE]; gb = Gh_[ec % NE]; Sm = Sm_[ec % NE]; ec += 1
                            Ef = Eb[:].re("p a b -> p (a b)")
                            s2b = ss[ti][:, h, 1, :].re("p (o j) -> p o j", o=1).bc([128, 4, 128])
                            s1b = s1c[ti][:, h, pc * 4:pc * 4 + 4].re("p (i o) -> p i o", o=1).bc([128, 4, 128])
                            tt(fw, "dve" if h >= 6 else "pool", Sm[:], s2b, s1b, ALU.add)
                            yield
                            act(fw, Eb[:], Sm[:], AF.Exp)
                            yield
                            stt(fw, "dve", gb[:], Ef, stv[ti][:, h:h + 1], Ef, ALU.is_ge, ALU.mult)
                            yield
                            mm(fw, pG[z][:], ident_bf[:], gb[:], h == 0, h == 7)
                            yield
                        cp(fw, "act", GsR[gsi][:], pG[z][:])
                        yield

                    def t_body(tslot, gsi, ti, pc, ub, vb):
                        z = tslot
                        for k_ in range(8):
                            mm(fw, psc[z][:], h2[ti][:, k_, :], ub[:, k_, :], k_ == 0, k_ == 7)
                        yield
                        act(fw, gl[z][:], psc[z][:], AF.Gelu)
                        yield
                        tt(fw, "pool", Zt[z][:], gl[z][:], GsR[gsi][:], ALU.mult)
                        yield
                        for c_ in range(4):
                            tr(fw, ptz[z][:, c_ * 128:(c_ + 1) * 128], Zt[z][:, c_ * 128:(c_ + 1) * 128], ident_bf[:])
                        yield
                        cp(fw, "act", ZT[z][:].re("p c t -> p (c t)"), ptz[z][:])
                        yield
                        for nb in range(2):
                            for c_ in range(4):
                                mm(fw, pyy[z][:], ZT[z][:, c_, :], vb[:, c_, nb * 512:(nb + 1) * 512], c_ == 0, c_ == 3)
                            yield
                            ysl = ysb[ti][:, nb * 512:(nb + 1) * 512]
                            if pc == 0:
                                cp(fw, "dve", ysl, pyy[z][:])
                            else:
                                tt(fw, "dve", ysl, pyy[z][:], ysl, ALU.add)
                            yield

                    def it_list():
                        nonlocal pcnt
                        for pc in range(32):
                            e0 = pc * 512
                            ub = UTp[pcnt % 3]; vb = Vp[pcnt % 3]
                            fw.dma(ub[:], uT.v(uTv[:, :, e0:e0 + 512]), usem[pcnt % 3], queue="pool")
                            fw.dma(vb[:], ev.v(evv[:, pc * 4:pc * 4 + 4, :]), vsem[pcnt % 3], queue="pool")
                            pcnt += 1
                            for ti, tl in enumerate(tiles):
                                yield (ti, pc, ub, vb)
                    pending = it_list()
                    actG = {}; actT = {}; readyT = []
                    freeG = [0, 1]; freeT = [0, 1]
                    done = False
                    while True:
                        while freeG and not done and (len(actG) + len(readyT) + len(actT)) < NGS:
                            try:
                                a_ = next(pending)
                            except StopIteration:
                                done = True
                                break
                            s_ = freeG.pop(0)
                            gsi = itc % NGS; itc += 1
                            actG[s_] = (g_body(s_, gsi, a_[0], a_[1]), gsi, a_)
                        while freeT and readyT:
                            gsi, a_ = readyT.pop(0)
                            s_ = freeT.pop(0)
                            actT[s_] = t_body(s_, gsi, *a_)
                        if not actG and not actT and not readyT:
                            break
                        for s_ in list(actG.keys()):
                            try:
                                next(actG[s_][0])
                            except StopIteration:
                                _, gsi, a_ = actG.pop(s_)
                                freeG.append(s_)
                                readyT.append((gsi, a_))
                        for s_ in list(actT.keys()):
                            try:
                                next(actT[s_])
                            except StopIteration:
                                del actT[s_]
                                freeT.append(s_)
                    for ti, tl in enumerate(tiles):
                        fw.dma(x1l[:], X1_d.v(X1_d.ap()[tl * 128:(tl + 1) * 128, :], tl), x1ls)
                        tt(fw, "pool", tmpf[:], ysb[ti][:], g2_bc[:], ALU.mult)
                        tt(fw, "pool", tmpf[:], tmpf[:], x1l[:], ALU.add)
                        act(fw, junk[:], tmpf[:], AF.Square, accum=ssq[:])
                        act(fw, rstd[:], ssq[:], AF.Sqrt, bias=eps_col[:], scale=1.0 / D)
                        recip(fw, rstd[:], rstd[:])
                        ob_ = outs[tl % 2]
                        stt(fw, "dve", ob_[:], tmpf[:], rstd[:], fnw_s[:], ALU.mult, ALU.mult)
                        fw.dma(out.v(out.ap()[tl * 128:(tl + 1) * 128, :]), ob_[:], outsem[tl % 2])
                fw.phase_end()
            fw.stack = st

        fw.barrier()
        fw.replay(block)
        print("sems", getattr(fw, "n_sems", 0), "instrs", fw.n_instr, "waits", fw.n_wait, {k: len(v) for k, v in fw.lists.items()})
    return nc, inp, out, dbg_t, dict(QT_d=QT_d, KT_d=KT_d, Ktm_d=Ktm_d, Vtm_d=Vtm_d, GA_d=GA_d, GB_d=GB_d, OA_d=OA_d)


A_QKV = 3072; A_GATE = 1024; A_AB = 32
OFF_GA = 3072; OFF_AB = 4096; OFF_QB = 4128; OFF_F = 5152; OFF_IB = 7200; OFF_GB = 8224; OFF_MG = 9248


def prep_core(inp, core, W):
    b, hf = core // 2, core % 2
    hs = [4 * hf + i for i in range(4)]
    S = 128 * W
    x = inp["x"][b]; ctx = inp["ctx"][b]
    f = np.float32
    d = {}
    d["xa"] = np.ascontiguousarray(np.concatenate([ctx, x], 0))
    SH = S // 2
    d["xh"] = np.ascontiguousarray(x[hf * SH:(hf + 1) * SH])
    hs_ = np.zeros((128, 2), f); hs_[:, hf] = 1.0
    d["hsel"] = hs_
    cc = np.stack([inp["c"][b].reshape(8, 128).T, inp["c_ctx"].reshape(8, 128).T], -1)
    d["ccol"] = np.ascontiguousarray(cc.reshape(128, 16))
    d["w_ada"] = inp["w_ada"][0]
    d["b_adaT"] = np.ascontiguousarray(inp["b_ada"][0].reshape(48, 128).T)
    d["b_row"] = np.ascontiguousarray(inp["b_ada"][0].reshape(1, -1))
    d["n1"] = np.ascontiguousarray(inp["norm1_w"][0].reshape(8, 128).T)
    d["n2"] = np.ascontiguousarray(inp["norm2_w"][0].reshape(8, 128).T)
    d["fnw"] = np.ascontiguousarray(np.broadcast_to(inp["final_norm_w"], (128, 1024)))
    w_in = inp["w_in"][0]
    hc = lambda off: np.concatenate([np.arange(off + h * 128, off + (h + 1) * 128) for h in hs])
    colsA = np.concatenate([hc(0), hc(1024), hc(2048)])
    d["wA"] = np.ascontiguousarray(w_in[:, colsA])
    d["wGA"] = np.ascontiguousarray(w_in[:, hc(OFF_GA)])
    ab_cols = np.array([OFF_AB + seg * 8 + h for seg in range(4) for h in hs])
    d["wAB"] = np.ascontiguousarray(w_in[:, ab_cols])
    d["wBq"] = np.ascontiguousarray(w_in[:, hc(OFF_QB)])
    d["wBf"] = np.ascontiguousarray(w_in[:, np.concatenate([hc(OFF_F), hc(OFF_F + 1024)])])
    d["wBv"] = np.ascontiguousarray(w_in[:, hc(OFF_IB)])
    d["wBg"] = np.ascontiguousarray(w_in[:, hc(OFF_GB)])
    d["wMG"] = np.ascontiguousarray(w_in[:, OFF_MG:])
    cw = inp["conv_w"][0][:, colsA]
    d["convw"] = np.ascontiguousarray(cw.reshape(3, 12, 128).transpose(2, 1, 0).reshape(128, 36))
    gp = np.concatenate([inp["dt_bias"][0][0, hs], inp["dt_bias"][0][1, hs], inp["a_log"][0][0, hs], inp["a_log"][0][1, hs]])
    d["gpar"] = np.ascontiguousarray(np.broadcast_to(gp.astype(f), (128, 16)))
    d["gnw"] = np.ascontiguousarray(np.broadcast_to(inp["gdn_norm_w"][0], (128, 128)))
    d["hnw"] = np.ascontiguousarray(np.broadcast_to(inp["hg_norm_w"][0], (128, 128)))
    lb = inp["lb_logits"]
    d["lbl"] = np.ascontiguousarray(np.concatenate([lb[0, hc(0)].reshape(4, 128).T, lb[1, hc(0)].reshape(4, 128).T], 1))
    d["w_pa"] = inp["w_pa"][0]; d["w_pb"] = inp["w_pb"][0]; d["w_o"] = inp["w_o"][0]
    d["w_q"] = inp["w_query"][0]
    sk = inp["sub_keys"][0]
    d["skT"] = np.ascontiguousarray(sk.reshape(16, 128, 128).transpose(2, 0, 1).reshape(128, 16 * 128))
    d["uT"] = inp["_uT"]
    d["ev"] = inp["expert_v"][0]
    return {k: np.ascontiguousarray(v, dtype=f) for k, v in d.items()}


_W = 64
_CACHE = {}


def kernel(**inputs):
    inp = {k: np.asarray(v) for k, v in inputs.items()}
    inp["_uT"] = np.ascontiguousarray(inp["expert_u"][0].T)
    if "nc" not in _CACHE:
        _CACHE["nc"] = build(_W, ("all",), ())
    nc, inps, out, dbg_t, scr = _CACHE["nc"]
    in_maps = []
    for c in range(8):
        d = prep_core(inp, c, _W)
        in_maps.append({k: d[k] for k in inps})
    res = run_bass_kernel_spmd(nc, in_maps, core_ids=list(range(8)))
    S = 128 * _W
    SH = S // 2
    full = np.empty((4, S, 1024), np.float32)
    for c in range(8):
        b, hf = c // 2, c % 2
        full[b, hf * SH:(hf + 1) * SH] = np.asarray(res.results[c]["out"])
    return full
```

```python
from contextlib import ExitStack
import numpy as np
import concourse.bass as bass
import concourse.mybir as mybir
from concourse.bass_utils import run_bass_kernel_spmd

F32 = mybir.dt.float32
BF16 = mybir.dt.bfloat16
AF = mybir.ActivationFunctionType
ALU = mybir.AluOpType
AX = mybir.AxisListType


class Buf:
    __slots__ = ("name", "w", "r")

    def __init__(self, name=""):
        self.name = name
        self.w = None
        self.r = []


class T:
    def __init__(self, fw, tensor, name):
        self.t = tensor
        self.buf = Buf(name)
        self.name = name

    def __getitem__(self, key):
        return V(self.t[key], self.buf)

    def ap(self):
        return self.t.ap() if hasattr(self.t, "ap") and callable(getattr(self.t, "ap")) else self.t[:]


class V:
    __slots__ = ("ap", "buf")

    def __init__(self, ap, buf):
        self.ap = ap
        self.buf = buf

    def __getitem__(self, key):
        return V(self.ap[key], self.buf)

    def re(self, s, **kw):
        return V(self.ap.rearrange(s, **kw), self.buf)

    def bc(self, shape):
        return V(self.ap.to_broadcast(shape), self.buf)

    def with_ap(self, ap):
        return V(ap, self.buf)


def _ap(x):
    return x.ap if hasattr(x, "buf") or hasattr(x, "bufs") else x


class FW:
    ENG = ("pe", "act", "dve", "pool", "sp")

    def __init__(self, nc, stack):
        self.nc = nc
        self.stack = stack
        self.root = stack
        self.free_dsems = []
        self.phase_dsems = []
        self.h = {"pe": nc.tensor, "act": nc.scalar, "dve": nc.vector, "pool": nc.gpsimd, "sp": nc.sync}
        self.lists = {e: [] for e in self.ENG}
        self.EPOCH = 16000
        self.semobjs = {}
        self.cur = {}
        for e in ("pe", "act", "dve", "pool"):
            self._new_epoch(e, 0)
        self.cnt = {e: 0 for e in self.ENG}
        self.known = {e: {} for e in self.ENG}
        self.dsems = []
        self.n_instr = 0
        self.n_wait = 0
        self._dma_rr = 0
        self.uid = 0
        self.last = {}
        self.lastd = {}

    def _new_epoch(self, e, n):
        if not hasattr(self, "snaps"):
            self.snaps = {x: [] for x in ("pe", "act", "dve", "pool")}
            self._wsrc = {}
        key = f"{e}{n}"
        self._wsrc[key] = e
        sem = self.root.enter_context(self.nc.semaphore("s_" + key))
        self.semobjs[key] = sem
        self.cur[e] = [key, sem, 0, n]

    def sb(self, shape, dt, name=None):
        self.uid += 1
        name = name or f"sb{self.uid}"
        t = self.stack.enter_context(self.nc.sbuf_tensor(name, list(shape), dt))
        return T(self, t, name)

    def ps(self, shape, dt=F32, name=None):
        self.uid += 1
        name = name or f"ps{self.uid}"
        t = self.stack.enter_context(self.nc.psum_tensor(name, list(shape), dt))
        return T(self, t, name)

    def dsem(self, name=None):
        if self.free_dsems:
            ent = self.free_dsems.pop()
        else:
            self.uid += 1
            key = f"d{self.uid}"
            s = self.root.enter_context(self.nc.semaphore(key))
            ent = [s, 0, key]
            self.semobjs[key] = s
            self.n_sems = getattr(self, "n_sems", 0) + 1
        self.phase_dsems.append(ent)
        return ent

    def phase_end(self):
        self.barrier()
        self.free_dsems.extend(self.phase_dsems)
        self.phase_dsems = []

    def _need(self, eng, ev, waits):
        if ev is None:
            return
        key, val = ev[0], ev[1]
        if eng == "pe" and ev[2] == "pe":
            return
        if self.known[eng].get(key, 0) >= val:
            return
        if waits.get(key, 0) < val:
            waits[key] = val

    def _gidx(self, key, val, e):
        return int(key[len(e):]) * self.EPOCH + val

    def _emit_waits(self, eng, waits, semobjs):
        if not waits:
            return
        kn = self.known[eng]
        for key, val in waits.items():
            sem = semobjs[key]
            kn[key] = val
            self.n_wait += 1
            self.lists[eng].append(("w", sem, val))
            src = self._wsrc.get(key)
            if src is not None:
                snaps = self.snaps[src]
                g = self._gidx(key, val, src)
                lo, hi = 0, len(snaps)
                while lo < hi:
                    mid = (lo + hi) // 2
                    if snaps[mid][0] < g:
                        lo = mid + 1
                    else:
                        hi = mid
                if lo > 0:
                    for k2, v2 in snaps[lo - 1][1].items():
                        if kn.get(k2, 0) < v2:
                            kn[k2] = v2
        if eng in self.snaps:
            c = self.cur[eng]
            self.snaps[eng].append((c[3] * self.EPOCH + c[2], dict(kn)))

    def op(self, eng, fn, reads=(), writes=()):
        waits = {}
        semobjs = self._semobjs
        rb = [x.buf if isinstance(x, (V, T)) else x for x in reads]
        wb = [x.buf if isinstance(x, (V, T)) else x for x in writes]
        for b in rb:
            self._need(eng, b.w, waits)
        for b in wb:
            self._need(eng, b.w, waits)
            for ev in b.r:
                self._need(eng, ev, waits)
        self._emit_waits(eng, waits, semobjs)
        c = self.cur[eng]
        if c[2] >= self.EPOCH:
            self._new_epoch(eng, c[3] + 1)
            c = self.cur[eng]
        c[2] += 1
        ev = (c[0], c[2], eng)
        self.lists[eng].append(("i", fn, c[1]))
        self.last[eng] = ev
        self.n_instr += 1
        for b in wb:
            b.w = ev
            b.r = []
        for b in rb:
            if b in wb:
                continue
            b.r = [e for e in b.r if e[2] != eng] + [ev]
        return ev

    @property
    def _semobjs(self):
        return self.semobjs

    def dma(self, out, in_, dsem, queue="sp", reads=None, writes=None, **kw):
        eng = queue
        waits = {}
        rb = [x.buf for x in (reads if reads is not None else [in_])]
        wb = [x.buf for x in (writes if writes is not None else [out])]
        for b in rb:
            self._need(eng, b.w, waits)
        for b in wb:
            self._need(eng, b.w, waits)
            for ev in b.r:
                self._need(eng, ev, waits)
        self._emit_waits(eng, waits, self._semobjs)
        if dsem[1] >= self.EPOCH:
            self.uid += 1
            key = f"d{self.uid}"
            dsem[0] = self.root.enter_context(self.nc.semaphore(key))
            dsem[1] = 0
            dsem[2] = key
            self.semobjs[key] = dsem[0]
        dsem[1] += 16
        ev = (dsem[2], dsem[1], None)
        o_ap, i_ap = _ap(out), _ap(in_)
        self.lists[eng].append(("d", o_ap, i_ap, dsem[0], kw))
        self.lastd[dsem[2]] = ev
        self.n_instr += 1
        for b in wb:
            b.w = ev
            b.r = []
        for b in rb:
            if b in wb:
                continue
            b.r = b.r + [ev]
        return ev

    def wait_all(self, eng, evs):
        waits = {}
        for ev in evs:
            self._need(eng, ev, waits)
        self._emit_waits(eng, waits, self._semobjs)

    def barrier(self):
        evs = list(self.last.values()) + list(self.lastd.values())
        for e in self.ENG:
            self.wait_all(e, evs)
        self.lastd = {}

    def replay(self, block):
        lists = self.lists

        def run(eng_handle, items):
            for it in items:
                if it[0] == "w":
                    eng_handle.wait_ge(it[1], it[2])
                elif it[0] == "i":
                    it[1](eng_handle).then_inc(it[2], 1)
                else:
                    _, o, i, s, kw = it
                    eng_handle.dma_start(out=o, in_=i, **kw).then_inc(s, 16)

        @block.tensor
        def _(e):
            run(e, lists["pe"])

        @block.scalar
        def _(e):
            run(e, lists["act"])

        @block.vector
        def _(e):
            run(e, lists["dve"])

        @block.gpsimd
        def _(e):
            run(e, lists["pool"])

        @block.sync
        def _(e):
            run(e, lists["sp"])


def _bufs(*xs):
    return [x for x in xs if isinstance(x, (V, T))]


def mm(fw, out, lhsT, rhs, start=True, stop=True):
    o, l, r = out.ap, lhsT.ap, rhs.ap
    return fw.op("pe", lambda e: e.matmul(o, lhsT=l, rhs=r, start=start, stop=stop), [lhsT, rhs], [out])


def tr(fw, out, in_, ident):
    o, i, d = out.ap, in_.ap, ident.ap
    return fw.op("pe", lambda e: e.transpose(o, i, d), [in_, ident], [out])


def act(fw, out, in_, func, bias=None, scale=1.0, accum=None, eng="act"):
    o, i = out.ap, in_.ap
    kw = {}
    reads = [in_]
    writes = [out]
    if bias is not None:
        kw["bias"] = _ap(bias)
        reads += _bufs(bias)
    if isinstance(scale, V):
        reads.append(scale)
    kw["scale"] = _ap(scale)
    if accum is not None:
        kw["accum_out"] = accum.ap
        writes.append(accum)
    return fw.op("act", lambda e: e.activation(out=o, in_=i, func=func, **kw), reads, writes)


def tt(fw, eng, out, in0, in1, op):
    o, a, b = out.ap, in0.ap, in1.ap
    return fw.op(eng, lambda e: e.tensor_tensor(out=o, in0=a, in1=b, op=op), [in0, in1], [out])


def ts(fw, eng, out, in0, s1, s2, op0, op1=None, accum=None):
    o, a = out.ap, in0.ap
    kw = {}
    if op1 is not None:
        kw["op1"] = op1
    writes = [out]
    if accum is not None:
        kw["accum_out"] = accum.ap
        writes.append(accum)
    return fw.op(eng, lambda e: e.tensor_scalar(out=o, in0=a, scalar1=_ap(s1), scalar2=_ap(s2), op0=op0, **kw),
                 [in0] + _bufs(s1, s2), writes)


def stt(fw, eng, out, in0, scalar, in1, op0, op1):
    o, a, b = out.ap, in0.ap, in1.ap
    return fw.op(eng, lambda e: e.scalar_tensor_tensor(out=o, in0=a, scalar=_ap(scalar), in1=b, op0=op0, op1=op1),
                 [in0, in1] + _bufs(scalar), [out])


def cp(fw, eng, out, in_):
    o, i = out.ap, in_.ap
    if eng == "act":
        return fw.op("act", lambda e: e.activation(out=o, in_=i, func=AF.Copy), [in_], [out])
    return fw.op(eng, lambda e: e.tensor_copy(out=o, in_=i), [in_], [out])


def memset(fw, eng, out, val):
    o = out.ap
    return fw.op(eng, lambda e: e.memset(o, val), [], [out])


def recip(fw, out, in_):
    o, i = out.ap, in_.ap
    return fw.op("dve", lambda e: e.reciprocal(out=o, in_=i), [in_], [out])


D = 1024
EPS = 1e-6
U8 = mybir.dt.uint8


class DR:
    def __init__(self, nc, name, shape, dt, npieces=1, kind=None):
        if kind is None:
            self.t = nc.dram_tensor(name, list(shape), dt)
        else:
            self.t = nc.dram_tensor(name, list(shape), dt, kind=kind)
        self.bufs = [Buf(f"{name}{i}") for i in range(npieces)]
        self.all = self.bufs

    def v(self, ap, piece=0):
        return V(ap, self.bufs[piece])

    def ap(self):
        return self.t.ap()


class MV:
    def __init__(self, ap, bufs):
        self.ap = ap
        self.bufs = bufs


def build(W, stages=("all",), dbg=()):
    nc = bass.Bass("TRN2", target_bir_lowering=False)
    NT = W + 2
    NTOK = 128 * NT
    S = 128 * W
    SH = S // 2
    inp = {}

    def din(name, shape, dt=F32):
        inp[name] = DR(nc, name, shape, dt, kind="ExternalInput")
        return inp[name]

    xa = din("xa", [NTOK, D])
    ccol = din("ccol", [128, 16])
    w_ada = din("w_ada", [D, 6 * D]); b_adaT = din("b_adaT", [128, 48]); b_row = din("b_row", [1, 6 * D])
    n1 = din("n1", [128, 8]); n2 = din("n2", [128, 8]); fnw = din("fnw", [128, D])
    wA = din("wA", [D, 1536]); wGA = din("wGA", [D, 512]); wAB = din("wAB", [D, 16])
    wBq = din("wBq", [D, 512]); wBf = din("wBf", [D, 1024]); wBv = din("wBv", [D, 512]); wBg = din("wBg", [D, 512])
    wMG = din("wMG", [D, 2048])
    convw = din("convw", [128, 36]); gpar = din("gpar", [128, 16])
    gnw = din("gnw", [128, 128]); hnw = din("hnw", [128, 128]); lbl = din("lbl", [128, 8])
    xh = din("xh", [SH, D]); hsel = din("hsel", [128, 2])
    if "merge" in stages or "all" in stages:
        w_pa = din("w_pa", [D, D]); w_pb = din("w_pb", [D, D]); w_o = din("w_o", [D, D])
    if "peer" in stages or "all" in stages:
        w_q = din("w_q", [D, 2048]); skT = din("skT", [128, 16 * 128])
        uT = din("uT", [D, 16384]); ev = din("ev", [16384, D])
    out = DR(nc, "out", [SH, D], F32, kind="ExternalOutput")
    dbg_t = {}

    QT_d = DR(nc, "QT_d", [4, 128, NTOK], BF16, NT)
    KT_d = DR(nc, "KT_d", [4, 128, NTOK], BF16, NT)
    Ktm_d = DR(nc, "Ktm_d", [NTOK, 512], BF16, NT)
    Vtm_d = DR(nc, "Vtm_d", [NTOK, 512], BF16, NT)
    GA_d = DR(nc, "GA_d", [NTOK, 512], BF16, NT)
    GB_d = DR(nc, "GB_d", [NTOK, 16], F32, NT)
    OA_d = [DR(nc, f"OA_d{d}", [NTOK, 512], F32, NT) for d in range(2)]
    QE_d = [DR(nc, f"QE_d{d}", [4, 128, NTOK], BF16, NT) for d in range(2)]
    KE_d = [DR(nc, f"KE_d{d}", [4, 128, NTOK], BF16, NT) for d in range(2)]
    QD_d = [DR(nc, f"QD_d{d}", [4, 128, NTOK], BF16, NT) for d in range(2)]
    KDtm_d = [DR(nc, f"KDtm_d{d}", [NTOK, 512], BF16, NT) for d in range(2)]
    VB_d = DR(nc, "VB_d", [NTOK, 512], BF16, NT)
    GBG_d = DR(nc, "GBG_d", [NTOK, 512], BF16, NT)
    OB_d = [DR(nc, f"OB_d{d}", [NTOK, 512], F32, NT) for d in range(2)]
    XCH_d = DR(nc, "XCH_d", [S, 1024], BF16, 2 * W)
    XG_d = DR(nc, "XG_d", [2 * S, 1024], BF16, 1)
    BAR_s = DR(nc, "BAR_s", [64, 128], BF16, 1); BAR_g = DR(nc, "BAR_g", [128, 128], BF16, 1)
    NTL = SH // 128
    X1_d = DR(nc, "X1_d", [SH, D], F32, NTL)
    H2T_d = DR(nc, "H2T_d", [NTL, 128, D], BF16, NTL)
    SS_d = DR(nc, "SS_d", [SH, 2048], F32, NTL)
    ST_d = DR(nc, "ST_d", [SH, 16], F32, NTL)

    st = ExitStack()
    with st:
        fw = FW(nc, st)
        block = st.enter_context(nc.Block())

        def dbg_out(name, src_v, shape, dt=F32):
            if name not in dbg:
                return
            t = DR(nc, "dbg_" + name, shape, dt, kind="ExternalOutput")
            dbg_t[name] = t
            fw.dma(t.v(t.ap()), src_v, fw.dsem(), queue="sp")

        def dbg_dram(name, dr, shape, dt):
            if name not in dbg:
                return
            t = DR(nc, "dbg_" + name, shape, dt, kind="ExternalOutput")
            dbg_t[name] = t
            fw.dma(t.v(t.ap()), V(dr.ap(), dr.bufs[0]), fw.dsem(), queue="sp", reads=[V(None, b) for b in dr.bufs])

        ident_bf = fw.sb([128, 128], BF16); ident_f = fw.sb([128, 128], F32); ones_f = fw.sb([128, 128], F32)
        mF = fw.sb([128, 128], F32); mB = fw.sb([128, 128], F32); mFs = fw.sb([128, 128], F32); mBs = fw.sb([128, 128], F32)

        def sel(t, pat, cm, op):
            memset(fw, "pool", t[:], 1.0)
            o = t.t[:]
            fw.op("pool", lambda e: e.affine_select(out=o, in_=o, pattern=[[pat, 128]], compare_op=op, fill=0.0,
                                                    base=0, channel_multiplier=cm), [t], [t])
        sel(ident_f, -1, 1, ALU.is_equal)
        cp(fw, "pool", ident_bf[:], ident_f[:])
        memset(fw, "pool", ones_f[:], 1.0)
        sel(mF, 1, -1, ALU.is_ge)
        sel(mB, -1, 1, ALU.is_ge)
        sel(mFs, 1, -1, ALU.is_gt)
        sel(mBs, -1, 1, ALU.is_gt)
        MASK = [mF, mB]; MASKS = [mFs, mBs]
        eps_col = fw.sb([128, 1], F32); one_col = fw.sb([128, 1], F32)
        memset(fw, "pool", eps_col[:], EPS); memset(fw, "pool", one_col[:], 1.0)

        def load_small(dr, shape, dt=F32, queue="sp"):
            t = fw.sb(shape, dt)
            fw.dma(t[:], dr.v(dr.ap()), fw.dsem(), queue=queue)
            return t
        n1_s = load_small(n1, [128, 8]); n2_s = load_small(n2, [128, 8])
        convw_s = load_small(convw, [128, 36]); gpar_s = load_small(gpar, [128, 16])
        gnw_s = load_small(gnw, [128, 128]); hnw_s = load_small(hnw, [128, 128]); lbl_s = load_small(lbl, [128, 8])
        badaT_s = load_small(b_adaT, [128, 48]); ccol_s = load_small(ccol, [128, 16])

        modT = fw.sb([128, 48, 2], F32)
        g1_bc = fw.sb([128, D], F32); g2_bc = fw.sb([128, D], F32)
        A1 = fw.sb([128, 8, 2], F32); A2 = fw.sb([128, 8], F32)
        with ExitStack() as ph:
            fw.stack = ph
            sc = fw.sb([128, 16], F32)
            act(fw, sc[:], ccol_s[:], AF.Silu)
            scv = sc[:].re("p (k w) -> p k w", w=2)
            wts = [fw.sb([128, 8, 768], F32) for _ in range(2)]
            wsem = [fw.dsem() for _ in range(2)]
            psm = fw.ps([128, 96])
            wv = w_ada.ap().rearrange("(k p) n -> p k n", p=128)
            for blk in range(8):
                wt = wts[blk % 2]
                fw.dma(wt[:], w_ada.v(wv[:, :, blk * 768:(blk + 1) * 768]), wsem[blk % 2])
                for jj in range(6):
                    j = blk * 6 + jj
                    for k in range(8):
                        mm(fw, psm[:, 2 * j:2 * j + 2], wt[:, k, jj * 128:(jj + 1) * 128], scv[:, k, :], k == 0, k == 7)
            tt(fw, "dve", modT[:], psm[:].re("p (j w) -> p j w", w=2), badaT_s[:].re("p (j o) -> p j o", o=1).bc([128, 48, 2]), ALU.add)
            stt(fw, "dve", A1[:], modT[:, 8:16, :], 1.0, n1_s[:].re("p (k o) -> p k o", o=1).bc([128, 8, 2]), ALU.add, ALU.mult)
            stt(fw, "dve", A2[:], modT[:, 32:40, 0], 1.0, n2_s[:], ALU.add, ALU.mult)
            brow = fw.sb([1, 2 * D], F32)
            fw.dma(brow[:, 0:D], b_row.v(b_row.ap()[:, 2 * D:3 * D]), fw.dsem())
            fw.dma(brow[:, D:2 * D], b_row.v(b_row.ap()[:, 5 * D:6 * D]), fw.dsem())
            grow = fw.sb([2, 2 * D], F32)
            wg = fw.sb([128, 8, D], F32)
            wgs = fw.dsem()
            psr = fw.ps([128, 512])
            for gi, c0 in enumerate((2 * D, 5 * D)):
                fw.dma(wg[:], w_ada.v(wv[:, :, c0:c0 + D]), wgs)
                for hb in range(2):
                    for k in range(8):
                        mm(fw, psr[0:2, :], scv[:, k, :], wg[:, k, hb * 512:(hb + 1) * 512], k == 0, k == 7)
                    tt(fw, "dve", grow[0:1, gi * D + hb * 512: gi * D + (hb + 1) * 512], psr[0:1, :],
                       brow[0:1, gi * D + hb * 512: gi * D + (hb + 1) * 512], ALU.add)
            for gi, gbc in enumerate((g1_bc, g2_bc)):
                for hb in range(2):
                    mm(fw, psr[:], ones_f[0:1, :], grow[0:1, gi * D + hb * 512: gi * D + (hb + 1) * 512])
                    cp(fw, "act", gbc[:, hb * 512:(hb + 1) * 512], psr[:])
            dbg_out("modT", modT[:], [128, 48, 2])
            dbg_out("g1bc", g1_bc[:], [128, D])
            fw.phase_end()
        fw.stack = st

        def make_hT(xsrc_v, hT_dst, which, A, Sh, xt, junk, ssq, rstd, xn, pst):
            fw.dma(xt[:], xsrc_v, xt_sem[id(xt)])
            act(fw, junk[:], xt[:], AF.Square, accum=ssq[:])
            act(fw, rstd[:], ssq[:], AF.Sqrt, bias=eps_col[:], scale=1.0 / D)
            recip(fw, rstd[:], rstd[:])
            ts(fw, "dve", xn[:], xt[:], rstd[:], None, ALU.mult)
            for k in range(8):
                tr(fw, pst[:, k * 128:(k + 1) * 128], xn[:, k * 128:(k + 1) * 128], ident_bf[:])
            pv = pst[:].re("p (k t) -> p k t", k=8)
            tt(fw, "dve", hT_dst, pv, A.re("p (k o) -> p k o", o=1).bc([128, 8, 128]), ALU.mult)
            tt(fw, "pool", hT_dst, hT_dst, Sh.re("p (k o) -> p k o", o=1).bc([128, 8, 128]), ALU.add)

        xt_sem = {}

        if "aprep" in stages or "all" in stages:
            with ExitStack() as ph:
                fw.stack = ph
                WA = fw.sb([128, 8, 1536], BF16); WGA = fw.sb([128, 8, 512], BF16); WAB = fw.sb([128, 8, 16], BF16)
                fw.dma(WA[:], wA.v(wA.ap().rearrange("(k p) n -> p k n", p=128)), fw.dsem(), queue="pool")
                fw.dma(WGA[:], wGA.v(wGA.ap().rearrange("(k p) n -> p k n", p=128)), fw.dsem(), queue="pool")
                fw.dma(WAB[:], wAB.v(wAB.ap().rearrange("(k p) n -> p k n", p=128)), fw.dsem(), queue="pool")
                xts = [fw.sb([128, D], F32) for _ in range(2)]
                for x_ in xts:
                    xt_sem[id(x_)] = fw.dsem()
                junk = fw.sb([128, D], BF16); ssq = fw.sb([128, 1], F32); rstd = fw.sb([128, 1], F32)
                xn = fw.sb([128, D], BF16)
                pst = [fw.ps([128, D], BF16) for _ in range(2)]
                hT = [fw.sb([128, 8, 512], BF16) for _ in range(2)]
                RAW = [fw.sb([128, 12, 514], F32) for _ in range(3)]
                CV = fw.sb([128, 12, 512], F32)
                SQ = fw.sb([128, 512], F32); RN = fw.sb([128, 512], F32)
                SQb = fw.sb([128, 512], F32); RNb = fw.sb([128, 512], F32)
                QK16 = fw.sb([128, 8, 512], BF16); V16 = fw.sb([128, 4, 512], BF16)
                psq = [fw.ps([128, 512]) for _ in range(2)]
                psg = fw.ps([128, 512])
                psab = fw.ps([128, 16])
                pstm = fw.ps([128, 512], BF16)
                ga_s = fw.sb([128, 512], BF16); gb_s = fw.sb([128, 16], F32); tmp16 = fw.sb([128, 16], F32)
                ktm_l = [fw.sb([128, 512], BF16) for _ in range(2)]; vtm_l = [fw.sb([128, 512], BF16) for _ in range(2)]
                pstm2 = fw.ps([128, 512], BF16)
                sem_st = {k: fw.dsem() for k in ("ga", "gb", "k0", "k1", "v0", "v1", "q", "kt")}
                negA = fw.sb([128, 8], F32)
                act(fw, negA[:], gpar_s[:, 8:16], AF.Exp)
                ts(fw, "dve", negA[:], negA[:], -1.0, None, ALU.mult)

                groups = [(0, 0, 2)] + [(1, 2 + 4 * g, 4) for g in range(W // 4)]
                tcount = 0
                qcount = 0

                def finish_group(gi, zero_next):
                    nonlocal qcount
                    seq, t0, ntl = groups[gi]
                    n = 128 * ntl
                    R = RAW[gi % 3]
                    if zero_next:
                        memset(fw, "pool", R[:, :, n + 1:n + 2], 0.0)
                    cwv = convw_s[:].re("p (m j) -> p m j", j=3)
                    for m in range(12):
                        if m % 3 != 2:
                            ts(fw, "dve", CV[:, m, 0:n], R[:, m, 0:n], cwv[:, m, 0:1], None, ALU.mult)
                            stt(fw, "dve", CV[:, m, 0:n], R[:, m, 1:n + 1], cwv[:, m, 1:2], CV[:, m, 0:n], ALU.mult, ALU.add)
                            stt(fw, "dve", CV[:, m, 0:n], R[:, m, 2:n + 2], cwv[:, m, 2:3], CV[:, m, 0:n], ALU.mult, ALU.add)
                        else:
                            ts(fw, "pool", CV[:, m, 0:n], R[:, m, 0:n], cwv[:, m, 0:1], None, ALU.mult)
                            for j_ in (1, 2):
                                ts(fw, "pool", SQ[:, 0:n], R[:, m, j_:n + j_], cwv[:, m, j_:j_ + 1], None, ALU.mult)
                                tt(fw, "pool", CV[:, m, 0:n], CV[:, m, 0:n], SQ[:, 0:n], ALU.add)
                    act(fw, CV[:, :, 0:n], CV[:, :, 0:n], AF.Silu)
                    for m in range(8):
                        SQ2 = SQ if m % 2 == 0 else SQb
                        RN2 = RN if m % 2 == 0 else RNb
                        tt(fw, "pool", SQ2[:, 0:n], CV[:, m, 0:n], CV[:, m, 0:n], ALU.mult)
                        pq = psq[qcount % 2]; qcount += 1
                        mm(fw, pq[:, 0:n], ones_f[:], SQ2[:, 0:n])
                        act(fw, RN2[:, 0:n], pq[:, 0:n], AF.Sqrt, bias=eps_col[:])
                        recip(fw, RN2[:, 0:n], RN2[:, 0:n])
                        scl = (128.0 ** -0.5) if m < 4 else 1.0
                        stt(fw, "dve", QK16[:, m, 0:n], CV[:, m, 0:n], scl, RN2[:, 0:n], ALU.mult, ALU.mult)
                    cp(fw, "act", V16[:, :, 0:n], CV[:, 8:12, 0:n])
                    tok0 = 128 * t0
                    fw.dma(MV(QT_d.ap().rearrange("h d t -> d h t")[:, :, tok0:tok0 + n], QT_d.bufs[t0:t0 + ntl]),
                           QK16[:, 0:4, 0:n], sem_st["q"], writes=[V(None, b) for b in QT_d.bufs[t0:t0 + ntl]])
                    fw.dma(MV(KT_d.ap().rearrange("h d t -> d h t")[:, :, tok0:tok0 + n], KT_d.bufs[t0:t0 + ntl]),
                           QK16[:, 4:8, 0:n], sem_st["kt"], writes=[V(None, b) for b in KT_d.bufs[t0:t0 + ntl]])
                    for tl in range(ntl):
                        for wi, (src, mb, dst_l, dst_d, key) in enumerate(((QK16, 4, ktm_l, Ktm_d, "k"), (V16, 0, vtm_l, Vtm_d, "v"))):
                            pt_ = pstm if wi == 0 else pstm2
                            dst_s = dst_l[tl % 2]
                            for h in range(4):
                                tr(fw, pt_[:, h * 128:(h + 1) * 128], src[:, mb + h, tl * 128:(tl + 1) * 128], ident_bf[:])
                            cp(fw, "act", dst_s[:], pt_[:])
                            tk = t0 + tl
                            fw.dma(dst_d.v(dst_d.ap()[tk * 128:(tk + 1) * 128, :], tk), dst_s[:], sem_st[key + str(tl % 2)])

                for gi, (seq, t0, ntl) in enumerate(groups):
                    n = 128 * ntl
                    h_ = hT[gi % 2]
                    R = RAW[gi % 3]
                    for tl in range(ntl):
                        tk = t0 + tl
                        xt = xts[tcount % 2]; ps_ = pst[tcount % 2]; tcount += 1
                        w = 1 if seq == 0 else 0
                        make_hT(xa.v(xa.ap()[tk * 128:(tk + 1) * 128, :]), h_[:, :, tl * 128:(tl + 1) * 128], w,
                                A1[:, :, w], modT[:, 0:8, w], xt, junk, ssq, rstd, xn, ps_)
                        for k in range(8):
                            mm(fw, psg[:], h_[:, k, tl * 128:(tl + 1) * 128], WGA[:, k, :], k == 0, k == 7)
                        act(fw, ga_s[:], psg[:], AF.Silu)
                        fw.dma(GA_d.v(GA_d.ap()[tk * 128:(tk + 1) * 128, :], tk), ga_s[:], sem_st["ga"])
                        for k in range(8):
                            mm(fw, psab[:], h_[:, k, tl * 128:(tl + 1) * 128], WAB[:, k, :], k == 0, k == 7)
                        tt(fw, "dve", tmp16[:, 0:8], psab[:, 0:8], gpar_s[:, 0:8], ALU.add)
                        act(fw, tmp16[:, 0:8], tmp16[:, 0:8], AF.Exp)
                        act(fw, tmp16[:, 0:8], tmp16[:, 0:8], AF.Ln, bias=one_col[:])
                        tt(fw, "dve", gb_s[:, 0:8], tmp16[:, 0:8], negA[:], ALU.mult)
                        act(fw, gb_s[:, 8:16], psab[:, 8:16], AF.Sigmoid)
                        fw.dma(GB_d.v(GB_d.ap()[tk * 128:(tk + 1) * 128, :], tk), gb_s[:], sem_st["gb"])
                    for m in range(12):
                        pq = psq[qcount % 2]; qcount += 1
                        for k in range(8):
                            mm(fw, pq[:, 0:n], WA[:, k, m * 128:(m + 1) * 128], h_[:, k, 0:n], k == 0, k == 7)
                        cp(fw, "act" if m % 2 else "dve", R[:, m, 1:n + 1], pq[:, 0:n])
                    first_in_seq = (gi == 0) or (groups[gi - 1][0] != seq)
                    last_in_seq = (gi == len(groups) - 1) or (groups[gi + 1][0] != seq)
                    if first_in_seq:
                        memset(fw, "pool", R[:, :, 0:1], 0.0)
                    else:
                        Rp = RAW[(gi - 1) % 3]
                        npv = 128 * groups[gi - 1][2]
                        cp(fw, "pool", Rp[:, :, npv + 1:npv + 2], R[:, :, 1:2])
                        cp(fw, "pool", R[:, :, 0:1], Rp[:, :, npv:npv + 1])
                        finish_group(gi - 1, False)
                    if last_in_seq:
                        finish_group(gi, True)
                dbg_out("hT0", hT[0][:], [128, 8, 512], BF16)
                fw.phase_end()
                for nm, dr, sh, dt in (("QT", QT_d, [4, 128, NTOK], BF16), ("KT", KT_d, [4, 128, NTOK], BF16),
                                       ("Ktm", Ktm_d, [NTOK, 512], BF16), ("Vtm", Vtm_d, [NTOK, 512], BF16),
                                       ("GA", GA_d, [NTOK, 512], BF16), ("GB", GB_d, [NTOK, 16], F32)):
                    dbg_dram(nm, dr, sh, dt)
            fw.stack = st

        if "achain" in stages or "all" in stages:
            with ExitStack() as ph:
                fw.stack = ph
                order = [list(range(NT)), [1, 0] + list(range(NT - 1, 1, -1))]
                S32 = [fw.sb([128, 4, 128], F32) for _ in range(2)]
                S16 = [fw.sb([128, 4, 128], BF16) for _ in range(2)]
                for d_ in range(2):
                    memset(fw, "pool", S32[d_][:], 0.0)
                    memset(fw, "pool", S16[d_][:], 0.0)
                NB = 2
                qT_s = [[fw.sb([128, 4, 128], BF16) for _ in range(NB)] for _ in range(2)]
                kT_s = [[fw.sb([128, 4, 128], BF16) for _ in range(NB)] for _ in range(2)]
                v_s = [[fw.sb([128, 4, 128], BF16) for _ in range(NB)] for _ in range(2)]
                k_s = [[fw.sb([128, 4, 128], BF16) for _ in range(NB)] for _ in range(2)]
                gb_l = [[fw.sb([128, 16], F32) for _ in range(NB)] for _ in range(2)]
                lsem = [[[fw.dsem() for _ in range(5)] for _ in range(NB)] for _ in range(2)]
                gU = fw.sb([128, 4, 128], F32)
                psA = fw.ps([128, 512])
                psS = fw.ps([128, 16])
                psK = [fw.ps([128, 512]) for _ in range(2)]
                psI = [fw.ps([128, 512]) for _ in range(2)]
                psC = [fw.ps([128, 512]) for _ in range(2)]
                ngc = fw.sb([128, 4], F32); Gtm = fw.sb([128, 4], F32); nG = fw.sb([128, 4], F32)
                rem = fw.sb([128, 4], F32); krem = fw.sb([128, 4], F32); tot = fw.sb([128, 4], F32)
                Gbc = fw.sb([128, 4, 128], F32)
                E = fw.sb([128, 4, 128], F32); DTm = fw.sb([128, 4, 128], F32); DTs = fw.sb([128, 4, 128], F32)
                qdT = fw.sb([128, 4, 128], BF16); KD = fw.sb([128, 4, 128], BF16)
                Af = fw.sb([128, 4, 128], F32); ATf = fw.sb([128, 4, 128], F32)
                X = [fw.sb([128, 4, 128], F32) for _ in range(2)]; XT = [fw.sb([128, 4, 128], F32) for _ in range(2)]
                P = [fw.sb([128, 4, 128], F32) for _ in range(2)]
                P16 = fw.sb([128, 4, 128], BF16); QKT = fw.sb([128, 4, 128], BF16)
                Rr = fw.sb([128, 4, 128], BF16); VN = fw.sb([128, 4, 128], BF16)
                Osb = [fw.sb([128, 4, 128], F32) for _ in range(2)]
                osem = [fw.dsem() for _ in range(2)]
                identb4 = ident_f[:].re("p (o i) -> p o i", o=1).bc([128, 4, 128])

                def loads(s, d_):
                    cid = order[d_][s]
                    b = s % NB
                    tk = slice(cid * 128, (cid + 1) * 128)
                    fw.dma(qT_s[d_][b][:], QT_d.v(QT_d.ap().rearrange("h d t -> d h t")[:, :, tk], cid), lsem[d_][b][0])
                    fw.dma(kT_s[d_][b][:], KT_d.v(KT_d.ap().rearrange("h d t -> d h t")[:, :, tk], cid), lsem[d_][b][1])
                    fw.dma(v_s[d_][b][:].re("p h d -> p (h d)"), Vtm_d.v(Vtm_d.ap()[tk, :], cid), lsem[d_][b][2])
                    fw.dma(k_s[d_][b][:].re("p h d -> p (h d)"), Ktm_d.v(Ktm_d.ap()[tk, :], cid), lsem[d_][b][3])
                    fw.dma(gb_l[d_][b][:], GB_d.v(GB_d.ap()[tk, :], cid), lsem[d_][b][4])

                for d_ in range(2):
                    loads(0, d_)
                for s in range(NT):
                    for d_ in range(2):
                        if s + 1 < NT:
                            loads(s + 1, d_)
                        cid = order[d_][s]
                        b = s % NB
                        qT, kT, vt, kt, gb = qT_s[d_][b], kT_s[d_][b], v_s[d_][b], k_s[d_][b], gb_l[d_][b]
                        g4 = gb[:, d_ * 4:d_ * 4 + 4]; be4 = gb[:, 8 + d_ * 4:12 + d_ * 4]
                        last = 127 if d_ == 0 else 0
                        Mk = MASK[d_]; Mks = MASKS[d_]
                        tt(fw, "pool", gU[:], Mk[:].re("p (o i) -> p o i", o=1).bc([128, 4, 128]),
                           g4.re("p (u o) -> p u o", o=1).bc([128, 4, 128]), ALU.mult)
                        mm(fw, psA[:], ones_f[:], gU[:].re("p u i -> p (u i)"))
                        mm(fw, psS[:, 0:4], Mk[:], g4)
                        pA3 = psA[:].re("p (u i) -> p u i", u=4)
                        ts(fw, "dve", ngc[:], psS[:, 0:4], -1.0, None, ALU.mult)
                        act(fw, Gtm[:], psS[:, 0:4], AF.Exp)
                        ts(fw, "dve", nG[:], Gtm[:], -1.0, None, ALU.mult)
                        tt(fw, "dve", rem[:], pA3[:, :, last], ngc[:], ALU.add)
                        act(fw, krem[:], rem[:], AF.Exp)
                        act(fw, tot[:], pA3[:, :, last], AF.Exp)
                        act(fw, Gbc[:].re("p u i -> p (u i)"), psA[:], AF.Exp)
                        for u in range(4):
                            act(fw, E[:, u, :], pA3[:, u, :], AF.Exp, bias=ngc[:, u:u + 1])
                        stt(fw, "dve", DTm[:], E[:], 1.0, Mk[:].re("p (o i) -> p o i", o=1).bc([128, 4, 128]), ALU.min, ALU.mult)
                        stt(fw, "dve", DTs[:], E[:], 1.0, Mks[:].re("p (o i) -> p o i", o=1).bc([128, 4, 128]), ALU.min, ALU.mult)
                        tt(fw, "pool", qdT[:], qT[:], Gbc[:], ALU.mult)
                        tt(fw, "pool", KD[:], kt[:], krem[:].re("p (u o) -> p u o", o=1).bc([128, 4, 128]), ALU.mult)
                        pk, pq_ = psK[0], psK[1]
                        for u in range(4):
                            mm(fw, pk[:, u * 128:(u + 1) * 128], kT[:, u, :], kT[:, u, :])
                            mm(fw, pq_[:, u * 128:(u + 1) * 128], kT[:, u, :], qT[:, u, :])
                        pk3 = pk[:].re("p (u i) -> p u i", u=4); pq3 = pq_[:].re("p (u i) -> p u i", u=4)
                        for u in range(4):
                            stt(fw, "dve", Af[:, u, :], pk3[:, u, :], be4[:, u:u + 1], DTs[:, u, :], ALU.mult, ALU.mult)
                        tt(fw, "dve", QKT[:], pq3, DTm[:], ALU.mult)
                        pi = psI[0]
                        for u in range(4):
                            tr(fw, pi[:, u * 128:(u + 1) * 128], Af[:, u, :], ident_f[:])
                        cp(fw, "act", ATf[:].re("p u i -> p (u i)"), pi[:])
                        tt(fw, "pool", P[0][:], identb4, Af[:], ALU.subtract)
                        Xc, XTc = Af, ATf
                        pcur = 0
                        for lvl in range(1, 7):
                            nx, nxt_ = X[lvl % 2], XT[lvl % 2]
                            lastl = lvl == 6
                            pa, pb = psI[0], psI[1]
                            for u in range(4):
                                mm(fw, pb[:, u * 128:(u + 1) * 128], Xc[:, u, :], XTc[:, u, :])
                            cp(fw, "act", nxt_[:].re("p u i -> p (u i)"), pb[:])
                            if not lastl:
                                for u in range(4):
                                    mm(fw, pa[:, u * 128:(u + 1) * 128], XTc[:, u, :], Xc[:, u, :])
                                cp(fw, "dve", nx[:].re("p u i -> p (u i)"), pa[:])
                            pc_ = psI[0] if lastl else psC[0]
                            for u in range(4):
                                mm(fw, pc_[:, u * 128:(u + 1) * 128], nxt_[:, u, :], P[pcur][:, u, :])
                            if lastl:
                                tt(fw, "dve", P16[:].re("p u i -> p (u i)"), pc_[:], P[pcur][:].re("p u i -> p (u i)"), ALU.add)
                            else:
                                tt(fw, "pool" if False else "dve", P[1 - pcur][:].re("p u i -> p (u i)"), pc_[:], P[pcur][:].re("p u i -> p (u i)"), ALU.add)
                                pcur = 1 - pcur
                            Xc, XTc = nx, nxt_
                        Sf, Sb = S32[d_], S16[d_]
                        pc = psC[1]
                        for u in range(4):
                            mm(fw, pc[:, u * 128:(u + 1) * 128], kT[:, u, :], Sb[:, u, :])
                        pc3 = pc[:].re("p (u i) -> p u i", u=4)
                        for u in range(4):
                            stt(fw, "dve", Rr[:, u, :], pc3[:, u, :], nG[:, u:u + 1], vt[:, u, :], ALU.mult, ALU.add)
                        pv = psC[0]
                        for u in range(4):
                            mm(fw, pv[:, u * 128:(u + 1) * 128], P16[:, u, :], Rr[:, u, :])
                        pv3 = pv[:].re("p (u i) -> p u i", u=4)
                        tt(fw, "dve", VN[:], pv3, be4.re("p (u o) -> p u o", o=1).bc([128, 4, 128]), ALU.mult)
                        if cid >= 2:
                            po = psK[0]
                            for u in range(4):
                                mm(fw, po[:, u * 128:(u + 1) * 128], qdT[:, u, :], Sb[:, u, :], True, False)
                                mm(fw, po[:, u * 128:(u + 1) * 128], QKT[:, u, :], VN[:, u, :], False, True)
                            ob = Osb[s % 2]
                            cp(fw, "act", ob[:].re("p u i -> p (u i)"), po[:])
                            fw.dma(OA_d[d_].v(OA_d[d_].ap()[cid * 128:(cid + 1) * 128, :], cid), ob[:].re("p u i -> p (u i)"), osem[s % 2])
                        pu = psK[1]
                        for u in range(4):
                            mm(fw, pu[:, u * 128:(u + 1) * 128], KD[:, u, :], VN[:, u, :])
                        pu3 = pu[:].re("p (u i) -> p u i", u=4)
                        for u in range(4):
                            stt(fw, "dve", Sf[:, u, :], Sf[:, u, :], tot[:, u:u + 1], pu3[:, u, :], ALU.mult, ALU.add)
                        cp(fw, "act", Sb[:], Sf[:])
                dbg_out("S32f", S32[0][:], [128, 4, 128])
                dbg_out("S32b", S32[1][:], [128, 4, 128])
                fw.phase_end()
                dbg_dram("OAf", OA_d[0], [NTOK, 512], F32)
                dbg_dram("OAb", OA_d[1], [NTOK, 512], F32)
            fw.stack = st


        def xsrc(tk):
            if tk < 2:
                return xa.v(xa.ap()[tk * 128:(tk + 1) * 128, :])
            return xa.v(xa.ap()[256:, :].rearrange("(r w) d -> w r d", w=W)[tk - 2])

        TOT_s = [fw.sb([128, 4, NT], F32) for _ in range(2)]
        if "bprep" in stages or "all" in stages:
            with ExitStack() as ph:
                fw.stack = ph
                WQF = fw.sb([128, 8, 1536], BF16); WV = fw.sb([128, 8, 512], BF16); WG = fw.sb([128, 8, 512], BF16)
                kpn = lambda dr: dr.ap().rearrange("(k p) n -> p k n", p=128)
                fw.dma(WQF[:, :, 0:512], wBq.v(kpn(wBq)), fw.dsem(), queue="pool")
                fw.dma(WQF[:, :, 512:1536], wBf.v(kpn(wBf)), fw.dsem(), queue="pool")
                fw.dma(WV[:], wBv.v(kpn(wBv)), fw.dsem(), queue="pool")
                fw.dma(WG[:], wBg.v(kpn(wBg)), fw.dsem(), queue="pool")
                xts = [fw.sb([128, D], F32) for _ in range(2)]
                for x_ in xts:
                    xt_sem[id(x_)] = fw.dsem()
                junk = fw.sb([128, D], BF16); ssq = fw.sb([128, 1], F32); rstd = fw.sb([128, 1], F32)
                xn = fw.sb([128, D], BF16)
                pst = [fw.ps([128, D], BF16) for _ in range(2)]
                hT = [fw.sb([128, 8, 512], BF16) for _ in range(2)]
                psq = [fw.ps([128, 512]) for _ in range(2)]
                psg = [fw.ps([128, 512]) for _ in range(2)]
                pstm = fw.ps([128, 512], BF16)
                lb = fw.sb([128, 4], F32); oml = fw.sb([128, 4], F32)
                tt(fw, "dve", lb[:], lbl_s[:, 0:4], lbl_s[:, 4:8], ALU.subtract)
                act(fw, lb[:], lb[:], AF.Sigmoid)
                ts(fw, "dve", oml[:], lb[:], -1.0, 1.0, ALU.mult, ALU.add)
                Qs = fw.sb([128, 4, 512], F32)
                LF = [fw.sb([128, 4, 512], F32) for _ in range(2)]
                KK_ = [fw.sb([128, 4, 512], F32) for _ in range(2)]
                onesb = fw.sb([128, 4, 512], F32)
                memset(fw, "pool", onesb[:], 1.0)
                CUM = fw.sb([128, 4, 512], F32); BC = fw.sb([128, 4, 512], F32); D1 = fw.sb([128, 4, 512], F32)
                EX = fw.sb([128, 4, 512], F32)
                OFF = fw.sb([128, 4, 4], F32); TT_ = fw.sb([128, 4, 4], F32)
                OUT16 = [fw.sb([128, 4, 512], BF16) for _ in range(4)]
                v_s = fw.sb([128, 512], BF16); g_s = fw.sb([128, 512], BF16); kd_s = fw.sb([128, 512], BF16)
                sem_b = {k_: fw.dsem() for k_ in ("v", "g", "qe", "ke", "qd", "kd")}
                groups = [(0, 0, 2)] + [(1, 2 + 4 * g, 4) for g in range(W // 4)]
                tcount = 0; qc = 0
                for gi, (seq, t0, ntl) in enumerate(groups):
                    n = 128 * ntl
                    h_ = hT[gi % 2]
                    for tl in range(ntl):
                        tk = t0 + tl
                        xt = xts[tcount % 2]; ps_ = pst[tcount % 2]; tcount += 1
                        w = 1 if seq == 0 else 0
                        make_hT(xsrc(tk), h_[:, :, tl * 128:(tl + 1) * 128], w, A1[:, :, w], modT[:, 0:8, w], xt, junk, ssq, rstd, xn, ps_)
                        pg = psg[tl % 2]
                        for k_ in range(8):
                            mm(fw, pg[:], h_[:, k_, tl * 128:(tl + 1) * 128], WV[:, k_, :], k_ == 0, k_ == 7)
                        cp(fw, "act", v_s[:], pg[:])
                        fw.dma(VB_d.v(VB_d.ap()[tk * 128:(tk + 1) * 128, :], tk), v_s[:], sem_b["v"])
                        pg = psg[(tl + 1) % 2]
                        for k_ in range(8):
                            mm(fw, pg[:], h_[:, k_, tl * 128:(tl + 1) * 128], WG[:, k_, :], k_ == 0, k_ == 7)
                        act(fw, g_s[:], pg[:], AF.Silu)
                        fw.dma(GBG_d.v(GBG_d.ap()[tk * 128:(tk + 1) * 128, :], tk), g_s[:], sem_b["g"])
                    for m in range(12):
                        pq = psq[qc % 2]; qc += 1
                        for k_ in range(8):
                            mm(fw, pq[:, 0:n], WQF[:, k_, m * 128:(m + 1) * 128], h_[:, k_, 0:n], k_ == 0, k_ == 7)
                        hh = m % 4
                        if m < 4:
                            act(fw, Qs[:, hh, 0:n], pq[:, 0:n], AF.Silu)
                        else:
                            d_ = (m - 4) // 4
                            act(fw, LF[d_][:, hh, 0:n], pq[:, 0:n], AF.Sigmoid)
                            ts(fw, "dve", LF[d_][:, hh, 0:n], LF[d_][:, hh, 0:n], oml[:, hh:hh + 1], lb[:, hh:hh + 1], ALU.mult, ALU.add)
                            ts(fw, "pool", KK_[d_][:, hh, 0:n], LF[d_][:, hh, 0:n], -1.0, 1.0, ALU.mult, ALU.add)
                    for d_ in range(2):
                        act(fw, LF[d_][:, :, 0:n], LF[d_][:, :, 0:n], AF.Ln)
                        lf = LF[d_]
                        if n == 512:
                            segs = [(CUM[:].re("p h i -> p (h i)"), onesb[:].re("p h i -> p (h i)"), lf[:].re("p h i -> p (h i)"))]
                        else:
                            segs = [(CUM[:, hh, 0:n], onesb[:, hh, 0:n], lf[:, hh, 0:n]) for hh in range(4)]
                        for (o1, a0, a1) in segs:
                            fw.op("dve", lambda e, o1=o1.ap, a0=a0.ap, a1=a1.ap: e.tensor_tensor_scan(out=o1, data0=a0, data1=a1, initial=0.0, op0=ALU.mult, op1=ALU.add),
                                  [onesb, lf], [CUM])
                        c4 = CUM[:, :, 0:n].re("p h (t i) -> p h t i", i=128)
                        l4 = lf[:, :, 0:n].re("p h (t i) -> p h t i", i=128)
                        b4 = BC[:, :, 0:n].re("p h (t i) -> p h t i", i=128)
                        d4 = D1[:, :, 0:n].re("p h (t i) -> p h t i", i=128)
                        tt(fw, "dve", OFF[:, :, 0:ntl], c4[:, :, :, 0], l4[:, :, :, 0], ALU.subtract)
                        offb = OFF[:, :, 0:ntl].re("p h (t o) -> p h t o", o=1).bc([128, 4, ntl, 128])
                        tt(fw, "dve", b4, c4, offb, ALU.subtract)
                        if d_ == 1:
                            tt(fw, "dve", TT_[:, :, 0:ntl], b4[:, :, :, 127], b4[:, :, :, 127], ALU.max)
                            ttb = TT_[:, :, 0:ntl].re("p h (t o) -> p h t o", o=1).bc([128, 4, ntl, 128])
                            tt(fw, "dve", b4, ttb, b4, ALU.subtract)
                            tt(fw, "dve", b4, b4, l4, ALU.add)
                        lastp = 127 if d_ == 0 else 0
                        act(fw, TOT_s[d_][:, :, t0:t0 + ntl], b4[:, :, :, lastp], AF.Exp)
                        q3 = Qs[:, :, 0:n]; k3 = KK_[d_][:, :, 0:n]
                        sc_q = 128.0 ** -0.5
                        act(fw, EX[:, :, 0:n], BC[:, :, 0:n], AF.Exp)
                        stt(fw, "dve", OUT16[2][:, :, 0:n], q3, sc_q, EX[:, :, 0:n], ALU.mult, ALU.mult)
                        refb = b4[:, :, :, 64:65].bc([128, 4, ntl, 128])
                        tt(fw, "pool", d4, b4, refb, ALU.subtract)
                        act(fw, EX[:, :, 0:n], D1[:, :, 0:n], AF.Exp)
                        stt(fw, "dve", OUT16[0][:, :, 0:n], q3, sc_q, EX[:, :, 0:n], ALU.mult, ALU.mult)
                        act(fw, EX[:, :, 0:n], D1[:, :, 0:n], AF.Exp, scale=-1.0)
                        tt(fw, "pool", OUT16[1][:, :, 0:n], k3, EX[:, :, 0:n], ALU.mult)
                        lastb = b4[:, :, :, lastp:lastp + 1].bc([128, 4, ntl, 128])
                        tt(fw, "pool", d4, b4, lastb, ALU.subtract)
                        act(fw, EX[:, :, 0:n], D1[:, :, 0:n], AF.Exp, scale=-1.0)
                        tt(fw, "pool", OUT16[3][:, :, 0:n], k3, EX[:, :, 0:n], ALU.mult)
                        tok0 = 128 * t0
                        for (dr, src, key) in ((QE_d[d_], OUT16[0], "qe"), (KE_d[d_], OUT16[1], "ke"), (QD_d[d_], OUT16[2], "qd")):
                            fw.dma(MV(dr.ap().rearrange("h d t -> d h t")[:, :, tok0:tok0 + n], dr.bufs[t0:t0 + ntl]),
                                   src[:, :, 0:n], sem_b[key], writes=[V(None, b) for b in dr.bufs[t0:t0 + ntl]])
                        for tl in range(ntl):
                            for hh in range(4):
                                tr(fw, pstm[:, hh * 128:(hh + 1) * 128], OUT16[3][:, hh, tl * 128:(tl + 1) * 128], ident_bf[:])
                            cp(fw, "act", kd_s[:], pstm[:])
                            tk = t0 + tl
                            fw.dma(KDtm_d[d_].v(KDtm_d[d_].ap()[tk * 128:(tk + 1) * 128, :], tk), kd_s[:], sem_b["kd"])
                fw.phase_end()
                for d_ in range(2):
                    dbg_dram(f"QE{d_}", QE_d[d_], [4, 128, NTOK], BF16); dbg_dram(f"KE{d_}", KE_d[d_], [4, 128, NTOK], BF16)
                    dbg_dram(f"QD{d_}", QD_d[d_], [4, 128, NTOK], BF16); dbg_dram(f"KD{d_}", KDtm_d[d_], [NTOK, 512], BF16)
                    dbg_out(f"TOT{d_}", TOT_s[d_][:], [128, 4, NT])
                dbg_dram("VB", VB_d, [NTOK, 512], BF16)
            fw.stack = st

        if "bchain" in stages or "all" in stages:
            with ExitStack() as ph:
                fw.stack = ph
                order = [list(range(NT)), [1, 0] + list(range(NT - 1, 1, -1))]
                S32 = [fw.sb([128, 4, 128], F32) for _ in range(2)]
                S16 = [fw.sb([128, 4, 128], BF16) for _ in range(2)]
                for d_ in range(2):
                    memset(fw, "pool", S32[d_][:], 0.0)
                    memset(fw, "pool", S16[d_][:], 0.0)
                NB = 2
                mk = lambda dt=BF16: [[fw.sb([128, 4, 128], dt) for _ in range(NB)] for _ in range(2)]
                qe_s, ke_s, qd_s, kd_s2, vb_s = mk(), mk(), mk(), mk(), mk()
                lsem = [[[fw.dsem() for _ in range(5)] for _ in range(NB)] for _ in range(2)]
                psSC = [fw.ps([128, 512]) for _ in range(2)]
                psO = [fw.ps([128, 512]) for _ in range(2)]
                psU = [fw.ps([128, 512]) for _ in range(2)]
                SC16 = [fw.sb([128, 4, 128], BF16) for _ in range(2)]
                Osb = [fw.sb([128, 512], F32) for _ in range(2)]
                osem = [fw.dsem() for _ in range(2)]

                def loadsb(s, d_):
                    cid = order[d_][s]
                    b = s % NB
                    tk = slice(cid * 128, (cid + 1) * 128)
                    hd = lambda dr: dr.v(dr.ap().rearrange("h d t -> d h t")[:, :, tk], cid)
                    fw.dma(qe_s[d_][b][:], hd(QE_d[d_]), lsem[d_][b][0])
                    fw.dma(ke_s[d_][b][:], hd(KE_d[d_]), lsem[d_][b][1])
                    fw.dma(qd_s[d_][b][:], hd(QD_d[d_]), lsem[d_][b][2])
                    fw.dma(kd_s2[d_][b][:].re("p h d -> p (h d)"), KDtm_d[d_].v(KDtm_d[d_].ap()[tk, :], cid), lsem[d_][b][3])
                    fw.dma(vb_s[d_][b][:].re("p h d -> p (h d)"), VB_d.v(VB_d.ap()[tk, :], cid), lsem[d_][b][4])

                for d_ in range(2):
                    loadsb(0, d_)
                def bstep(s, d_):
                    if s + 1 < NT:
                        loadsb(s + 1, d_)
                    cid = order[d_][s]
                    b = s % NB
                    qe, ke, qd, kd, vb = qe_s[d_][b], ke_s[d_][b], qd_s[d_][b], kd_s2[d_][b], vb_s[d_][b]
                    Sf, Sb = S32[d_], S16[d_]
                    if cid >= 2:
                        psc = psSC[d_]
                        for u in range(4):
                            mm(fw, psc[:, u * 128:(u + 1) * 128], ke[:, u, :], qe[:, u, :])
                        yield
                        sc16 = SC16[d_]
                        tt(fw, "dve", sc16[:], psc[:].re("p (u i) -> p u i", u=4),
                           MASK[d_][:].re("p (o i) -> p o i", o=1).bc([128, 4, 128]), ALU.mult)
                        yield
                        po = psO[d_]
                        for u in range(4):
                            mm(fw, po[:, u * 128:(u + 1) * 128], qd[:, u, :], Sb[:, u, :], True, False)
                            mm(fw, po[:, u * 128:(u + 1) * 128], sc16[:, u, :], vb[:, u, :], False, True)
                        yield
                        ob = Osb[d_]
                        cp(fw, "act", ob[:], po[:])
                        fw.dma(OB_d[d_].v(OB_d[d_].ap()[cid * 128:(cid + 1) * 128, :], cid), ob[:], osem[d_])
                        yield
                    pu = psU[d_]
                    for u in range(4):
                        mm(fw, pu[:, u * 128:(u + 1) * 128], kd[:, u, :], vb[:, u, :])
                    yield
                    totb = TOT_s[d_][:, :, cid:cid + 1].bc([128, 4, 128])
                    tt(fw, "pool", Sf[:], Sf[:], totb, ALU.mult)
                    yield
                    tt(fw, "dve", Sf[:], Sf[:], pu[:].re("p (u i) -> p u i", u=4), ALU.add)
                    yield
                    cp(fw, "act", Sb[:], Sf[:])
                    yield

                for s in range(NT):
                    gens = [bstep(s, 0), bstep(s, 1)]
                    while gens:
                        for g_ in list(gens):
                            try:
                                next(g_)
                            except StopIteration:
                                gens.remove(g_)
                dbg_out("SBf", S32[0][:], [128, 4, 128])
                dbg_out("SBb", S32[1][:], [128, 4, 128])
                fw.phase_end()
                dbg_dram("OBf", OB_d[0], [NTOK, 512], F32)
                dbg_dram("OBb", OB_d[1], [NTOK, 512], F32)
            fw.stack = st

        if "fin" in stages or "all" in stages:
            with ExitStack() as ph:
                fw.stack = ph
                of_s = [fw.sb([128, 4, 128], F32) for _ in range(2)]; ob_s = [fw.sb([128, 4, 128], F32) for _ in range(2)]
                gt_s = [fw.sb([128, 4, 128], BF16) for _ in range(2)]
                fsem = [[fw.dsem() for _ in range(3)] for _ in range(2)]
                sq = fw.sb([128, 4, 128], F32); ssq4 = fw.sb([128, 4], F32); on = fw.sb([128, 4, 128], F32)
                y16 = [fw.sb([128, 512], BF16) for _ in range(2)]
                ysem = [fw.dsem() for _ in range(2)]
                cnt = 0
                for mixer in range(2):
                    Od = OA_d if mixer == 0 else OB_d
                    Gd = GA_d if mixer == 0 else GBG_d
                    nw = gnw_s if mixer == 0 else hnw_s
                    for c in range(W):
                        cid = 2 + c
                        b = cnt % 2; cnt += 1
                        tk = slice(cid * 128, (cid + 1) * 128)
                        fw.dma(of_s[b][:].re("p h d -> p (h d)"), Od[0].v(Od[0].ap()[tk, :], cid), fsem[b][0])
                        fw.dma(ob_s[b][:].re("p h d -> p (h d)"), Od[1].v(Od[1].ap()[tk, :], cid), fsem[b][1])
                        fw.dma(gt_s[b][:].re("p h d -> p (h d)"), Gd.v(Gd.ap()[tk, :], cid), fsem[b][2])
                        tt(fw, "pool", of_s[b][:], of_s[b][:], ob_s[b][:], ALU.add)
                        tt(fw, "pool", sq[:], of_s[b][:], of_s[b][:], ALU.mult)
                        fw.op("dve", lambda e, o=ssq4.t[:], i=sq.t[:]: e.tensor_reduce(out=o, in_=i, axis=AX.X, op=ALU.add), [sq], [ssq4])
                        act(fw, ssq4[:], ssq4[:], AF.Sqrt, bias=eps_col[:], scale=1.0 / 128)
                        recip(fw, ssq4[:], ssq4[:])
                        tt(fw, "dve", on[:], of_s[b][:], ssq4[:].re("p (h o) -> p h o", o=1).bc([128, 4, 128]), ALU.mult)
                        tt(fw, "pool", on[:], on[:], nw[:].re("p (o d) -> p o d", o=1).bc([128, 4, 128]), ALU.mult)
                        yb = y16[b]
                        tt(fw, "dve", yb[:].re("p (h d) -> p h d", h=4), on[:], gt_s[b][:], ALU.mult)
                        if mixer == 0:
                            dst = XCH_d.ap()[c * 128:(c + 1) * 128, 0:512]
                        else:
                            dst = XCH_d.ap().rearrange("(r w) d -> w r d", w=W)[c][:, 512:1024]
                        fw.dma(XCH_d.v(dst, mixer * W + c), yb[:], ysem[b])
                RB = min(512, S)
                for j_ in range(S // RB):
                    src_ap = XCH_d.ap()[j_ * RB:(j_ + 1) * RB, :].opt(); dst_ap = XG_d.ap()[j_ * 2 * RB:(j_ + 1) * 2 * RB, :].opt()
                    fw.op("pool", lambda e, src_ap=src_ap, dst_ap=dst_ap: e.collective_compute("AllGather", ALU.bypass, replica_groups=[[0, 1], [2, 3], [4, 5], [6, 7]],
                                                                 ins=[src_ap], outs=[dst_ap]), [V(None, b_) for b_ in XCH_d.bufs], [V(None, XG_d.bufs[0])])
                b_src = BAR_s.ap().opt(); b_dst = BAR_g.ap().opt()
                fw.op("pool", lambda e: e.collective_compute("AllGather", ALU.bypass, replica_groups=[[0, 1], [2, 3], [4, 5], [6, 7]],
                                                             ins=[b_src], outs=[b_dst]), [V(None, XG_d.bufs[0])], [V(None, XG_d.bufs[0])])
                fw.phase_end()
                dbg_dram("XG", XG_d, [2 * S, 1024], BF16)
            fw.stack = st

        NTL = SH // 128
        if "merge" in stages or "all" in stages:
            with ExitStack() as ph:
                fw.stack = ph
                kpn = lambda dr: dr.ap().rearrange("(k p) n -> p k n", p=128)
                Wmg = fw.sb([128, 8, 2048], BF16); Wpa = fw.sb([128, 8, D], BF16); Wpb = fw.sb([128, 8, D], BF16); Wo = fw.sb([128, 8, D], BF16)
                for wt_, dr_ in ((Wmg, wMG), (Wpa, w_pa), (Wpb, w_pb), (Wo, w_o)):
                    fw.dma(wt_[:], dr_.v(kpn(dr_)), fw.dsem(), queue="pool")
                xts = [fw.sb([128, D], F32) for _ in range(2)]
                for x_ in xts:
                    xt_sem[id(x_)] = fw.dsem()
                junk = fw.sb([128, D], BF16); ssq = fw.sb([128, 1], F32); rstd = fw.sb([128, 1], F32)
                xn = fw.sb([128, D], BF16)
                pst = [fw.ps([128, D], BF16) for _ in range(2)]
                psX = fw.ps([128, 2048])
                hT = fw.sb([128, 8, 128], BF16)
                gates = fw.sb([128, 2048], F32)
                oab = [fw.sb([128, 2, D], BF16) for _ in range(2)]
                osem = [[fw.dsem() for _ in range(4)] for _ in range(2)]
                oT = fw.sb([128, 2, 8, 128], BF16)
                Lx = [fw.sb([128, 2, 2, D], BF16) for _ in range(2)]
                selt = fw.sb([128, D], BF16)
                hsel_s = load_small(hsel, [128, 2])
                m1 = fw.sb([128, D], F32); m16 = fw.sb([128, D], BF16); mT = fw.sb([128, 8, 128], BF16)
                x1 = [fw.sb([128, D], F32) for _ in range(2)]; x1sem = [fw.dsem() for _ in range(2)]
                h2T = [fw.sb([128, 8, 128], BF16) for _ in range(2)]; h2sem = [fw.dsem() for _ in range(2)]
                rank_of_half = [0, 1]
                for tl in range(NTL):
                    b = tl % 2
                    xt = xts[b]
                    t0 = tl * 128
                    make_hT(xh.v(xh.ap()[t0:t0 + 128, :]), hT[:], 0, A1[:, :, 0], modT[:, 0:8, 0], xt, junk, ssq, rstd, xn, pst[b])
                    for r_ in range(2):
                        for hh in range(2):
                            tok_ = hh * SH + t0
                            RB = min(512, S)
                            row0 = (tok_ // RB) * 2 * RB + r_ * RB + tok_ % RB
                            fw.dma(Lx[b][:, r_, hh, :], XG_d.v(XG_d.ap()[row0:row0 + 128, :]), osem[b][r_ * 2 + hh])
                        ts(fw, "dve", selt[:], Lx[b][:, r_, 0, :], hsel_s[:, 0:1], None, ALU.mult)
                        stt(fw, "dve", selt[:], Lx[b][:, r_, 1, :], hsel_s[:, 1:2], selt[:], ALU.mult, ALU.add)
                        cp(fw, "pool", oab[b][:, :, r_ * 512:(r_ + 1) * 512], selt[:].re("p (m c) -> p m c", m=2))
                    for nb in range(4):
                        for k_ in range(8):
                            mm(fw, psX[:, nb * 512:(nb + 1) * 512], hT[:, k_, :], Wmg[:, k_, nb * 512:(nb + 1) * 512], k_ == 0, k_ == 7)
                    act(fw, gates[:], psX[:], AF.Sigmoid)
                    for mx in range(2):
                        pt = pst[(b + 1 + mx) % 2]
                        for k_ in range(8):
                            tr(fw, pt[:, k_ * 128:(k_ + 1) * 128], oab[b][:, mx, k_ * 128:(k_ + 1) * 128], ident_bf[:])
                        cp(fw, "act" if mx else "dve", oT[:, mx, :, :].re("p k t -> p (k t)"), pt[:])
                    for mx, Wp in ((0, Wpa), (1, Wpb)):
                        for nb in range(2):
                            for k_ in range(8):
                                mm(fw, psX[:, mx * 1024 + nb * 512: mx * 1024 + (nb + 1) * 512], oT[:, mx, k_, :], Wp[:, k_, nb * 512:(nb + 1) * 512], k_ == 0, k_ == 7)
                    tt(fw, "dve", m1[:], psX[:, 0:1024], gates[:, 0:1024], ALU.mult)
                    tt(fw, "dve", gates[:, 1024:2048], psX[:, 1024:2048], gates[:, 1024:2048], ALU.mult)
                    tt(fw, "pool", m16[:], m1[:], gates[:, 1024:2048], ALU.add)
                    pt = pst[b]
                    for k_ in range(8):
                        tr(fw, pt[:, k_ * 128:(k_ + 1) * 128], m16[:, k_ * 128:(k_ + 1) * 128], ident_bf[:])
                    cp(fw, "act", mT[:].re("p k t -> p (k t)"), pt[:])
                    for nb in range(2):
                        for k_ in range(8):
                            mm(fw, psX[:, nb * 512:(nb + 1) * 512], mT[:, k_, :], Wo[:, k_, nb * 512:(nb + 1) * 512], k_ == 0, k_ == 7)
                    tt(fw, "dve", m1[:], psX[:, 0:1024], g1_bc[:], ALU.mult)
                    tt(fw, "pool", x1[b][:], m1[:], xt[:], ALU.add)
                    fw.dma(X1_d.v(X1_d.ap()[t0:t0 + 128, :], tl), x1[b][:], x1sem[b])
                    if tl == 0:
                        dbg_out("d_x1", x1[b][:], [128, D]); dbg_out("d_m1", m1[:], [128, D]); dbg_out("d_gates", gates[:], [128, 2048])
                        dbg_out("d_oab", oab[b][:], [128, 2, D], BF16); dbg_out("d_Lx", Lx[b][:], [128, 2, 2, D], BF16); dbg_out("d_hsel", hsel_s[:], [128, 2]); dbg_out("d_hT", hT[:], [128, 8, 128], BF16); dbg_out("d_m16", m16[:], [128, D], BF16)
                    act(fw, junk[:], x1[b][:], AF.Square, accum=ssq[:])
                    act(fw, rstd[:], ssq[:], AF.Sqrt, bias=eps_col[:], scale=1.0 / D)
                    recip(fw, rstd[:], rstd[:])
                    ts(fw, "dve", xn[:], x1[b][:], rstd[:], None, ALU.mult)
                    pt = pst[(b + 1) % 2]
                    for k_ in range(8):
                        tr(fw, pt[:, k_ * 128:(k_ + 1) * 128], xn[:, k_ * 128:(k_ + 1) * 128], ident_bf[:])
                    pv = pt[:].re("p (k t) -> p k t", k=8)
                    tt(fw, "dve", h2T[b][:], pv, A2[:].re("p (k o) -> p k o", o=1).bc([128, 8, 128]), ALU.mult)
                    tt(fw, "pool", h2T[b][:], h2T[b][:], modT[:, 24:32, 0].re("p (k o) -> p k o", o=1).bc([128, 8, 128]), ALU.add)
                    fw.dma(H2T_d.v(H2T_d.ap()[tl].rearrange("p (k t) -> p k t", k=8), tl), h2T[b][:], h2sem[b])
                dbg_out("gates", gates[:], [128, 2048]); dbg_out("oab", oab[(NTL - 1) % 2][:], [128, 2, D], BF16)
                dbg_out("m1", m1[:], [128, D]); dbg_out("x1s", x1[(NTL - 1) % 2][:], [128, D]); dbg_out("xts", xts[(NTL - 1) % 2][:], [128, D]); dbg_out("hTm", hT[:], [128, 8, 128], BF16); dbg_out("Lx", Lx[(NTL - 1) % 2][:], [128, 2, 2, D], BF16)
                fw.phase_end()
                dbg_dram("x1", X1_d, [SH, D], F32)
            fw.stack = st

        if "peer" in stages or "all" in stages:
            with ExitStack() as ph:
                fw.stack = ph
                kpn = lambda dr: dr.ap().rearrange("(k p) n -> p k n", p=128)
                Wq = fw.sb([128, 8, 2048], BF16); skT_s = fw.sb([128, 16, 128], BF16)
                fw.dma(Wq[:], w_q.v(kpn(w_q)), fw.dsem(), queue="pool")
                fw.dma(skT_s[:].re("p a b -> p (a b)"), skT.v(skT.ap()), fw.dsem(), queue="pool")
                h2 = [fw.sb([128, 8, 128], BF16) for _ in range(2)]; h2s = [fw.dsem() for _ in range(2)]
                pqy = [fw.ps([128, 128]) for _ in range(2)]
                pss = fw.ps([128, 2048])
                qT = [fw.sb([128, 128], BF16) for _ in range(2)]
                s_sb = [fw.sb([128, 16, 128], F32) for _ in range(2)]; ssem = [fw.dsem() for _ in range(2)]
                T16 = fw.sb([128, 16, 16], F32); W1 = fw.sb([128, 128], F32)
                cand = fw.sb([128, 8, 256], F32); W2 = fw.sb([128, 256], F32); W3 = fw.sb([128, 256], F32)
                C24 = fw.sb([128, 8, 24], F32)
                stt_s = [fw.sb([128, 16], F32) for _ in range(2)]; stsem = [fw.dsem() for _ in range(2)]
                e16 = fw.sb([128, 8, 16], F32); Zs = fw.sb([128, 8], F32)

                def dve(f, reads, writes):
                    return fw.op("dve", f, reads, writes)
                for tl in range(NTL):
                    b = tl % 2
                    fw.dma(h2[b][:], H2T_d.v(H2T_d.ap()[tl].rearrange("p (k t) -> p k t", k=8), tl), h2s[b])
                    for hc in range(16):
                        pq = pqy[hc % 2]
                        for k_ in range(8):
                            mm(fw, pq[:], Wq[:, k_, hc * 128:(hc + 1) * 128], h2[b][:, k_, :], k_ == 0, k_ == 7)
                        cp(fw, "act" if hc % 2 else "dve", qT[hc % 2][:], pq[:])
                        mm(fw, pss[:, hc * 128:(hc + 1) * 128], qT[hc % 2][:], skT_s[:, hc, :])
                    sb_ = s_sb[b]
                    cp(fw, "act", sb_[:].re("p a b -> p (a b)"), pss[:])
                    fw.dma(SS_d.v(SS_d.ap()[tl * 128:(tl + 1) * 128, :], tl), sb_[:].re("p a b -> p (a b)"), ssem[b])
                    for hc in range(16):
                        dve(lambda e, o=T16.t[:, hc, 0:8], i=sb_.t[:, hc, :]: e.max(out=o, in_=i), [sb_], [T16])
                        dve(lambda e, o=W1.t[:], r=T16.t[:, hc, 0:8], i=sb_.t[:, hc, :]: e.match_replace(out=o, in_to_replace=r, in_values=i, imm_value=-1e30), [sb_, T16], [W1])
                        dve(lambda e, o=T16.t[:, hc, 8:16], i=W1.t[:]: e.max(out=o, in_=i), [W1], [T16])
                    t4 = T16[:].re("p (h c) a -> p h c a", c=2)
                    in0 = t4[:, :, 0, :].re("p h (a o) -> p h a o", o=1).bc([128, 8, 16, 16])
                    in1 = t4[:, :, 1, :].re("p h (o a) -> p h o a", o=1).bc([128, 8, 16, 16])
                    tt(fw, "pool", cand[:].re("p h (a c) -> p h a c", c=16), in0, in1, ALU.add)
                    for h in range(8):
                        dve(lambda e, o=C24.t[:, h, 0:8], i=cand.t[:, h, :]: e.max(out=o, in_=i), [cand], [C24])
                        dve(lambda e, o=W2.t[:], r=C24.t[:, h, 0:8], i=cand.t[:, h, :]: e.match_replace(out=o, in_to_replace=r, in_values=i, imm_value=-1e30), [cand, C24], [W2])
                        dve(lambda e, o=C24.t[:, h, 8:16], i=W2.t[:]: e.max(out=o, in_=i), [W2], [C24])
                        dve(lambda e, o=W3.t[:], r=C24.t[:, h, 8:16], i=W2.t[:]: e.match_replace(out=o, in_to_replace=r, in_values=i, imm_value=-1e30), [W2, C24], [W3])
                        dve(lambda e, o=C24.t[:, h, 16:24], i=W3.t[:]: e.max(out=o, in_=i), [W3], [C24])
                    sts = stt_s[b]
                    tt(fw, "pool", sts[:, 0:8], C24[:, :, 15], C24[:, :, 16], ALU.add)
                    ts(fw, "pool", sts[:, 0:8], sts[:, 0:8], 0.5, None, ALU.mult)
                    tt(fw, "pool", e16[:], C24[:, :, 0:16], C24[:, :, 0:1].bc([128, 8, 16]), ALU.subtract)
                    act(fw, e16[:], e16[:], AF.Exp)
                    dve(lambda e, o=Zs.t[:], i=e16.t[:]: e.tensor_reduce(out=o, in_=i, axis=AX.X, op=ALU.add), [e16], [Zs])
                    act(fw, Zs[:], Zs[:], AF.Ln)
                    tt(fw, "pool", Zs[:], Zs[:], C24[:, :, 0], ALU.add)
                    ts(fw, "pool", sts[:, 8:16], Zs[:], -1.0, None, ALU.mult)
                    tt(fw, "pool", sts[:, 0:8], sts[:, 0:8], sts[:, 8:16], ALU.add)
                    act(fw, sts[:, 0:8], sts[:, 0:8], AF.Exp)
                    fw.dma(ST_d.v(ST_d.ap()[tl * 128:(tl + 1) * 128, :], tl), sts[:], stsem[b])
                fw.phase_end()
                dbg_dram("SS", SS_d, [SH, 2048], F32); dbg_dram("ST", ST_d, [SH, 16], F32)
            fw.stack = st

        if "peer" in stages or "all" in stages:
            with ExitStack() as ph:
                fw.stack = ph
                GT = min(4, NTL)
                h2 = [fw.sb([128, 8, 128], BF16) for _ in range(GT)]
                ss = [fw.sb([128, 8, 2, 128], F32) for _ in range(GT)]
                s1c = [fw.sb([128, 8, 128], F32) for _ in range(GT)]
                stv = [fw.sb([128, 16], F32) for _ in range(GT)]
                ysb = [fw.sb([128, D], F32) for _ in range(GT)]
                lds = [[fw.dsem() for _ in range(3)] for _ in range(GT)]
                NE = 6
                Eb_ = [fw.sb([128, 4, 128], BF16) for _ in range(NE)]
                Gh_ = [fw.sb([128, 512], BF16) for _ in range(NE)]
                Sm_ = [fw.sb([128, 4, 128], F32) for _ in range(NE)]
                NGS = 5
                GsR = [fw.sb([128, 512], BF16) for _ in range(NGS)]
                itc = 0
                Pacc = [fw.sb([128, 512], BF16) for _ in range(2)]
                UTp = [fw.sb([128, 8, 512], BF16) for _ in range(3)]; Vp = [fw.sb([128, 4, D], BF16) for _ in range(3)]
                usem = [fw.dsem() for _ in range(3)]; vsem = [fw.dsem() for _ in range(3)]
                pG = [fw.ps([128, 512]) for _ in range(2)]
                psc = [fw.ps([128, 512]) for _ in range(2)]
                ptz = [fw.ps([128, 512], BF16) for _ in range(2)]
                pyy = [fw.ps([128, 512]) for _ in range(2)]
                gl = [fw.sb([128, 512], BF16) for _ in range(2)]; Zt = [fw.sb([128, 512], BF16) for _ in range(2)]
                ZT = [fw.sb([128, 4, 128], BF16) for _ in range(2)]
                x1l = fw.sb([128, D], F32); x1ls = fw.dsem()
                tmpf = fw.sb([128, D], F32); junk = fw.sb([128, D], BF16); ssq = fw.sb([128, 1], F32); rstd = fw.sb([128, 1], F32)
                fnw_s = fw.sb([128, D], F32)
                fw.dma(fnw_s[:], fnw.v(fnw.ap()), fw.dsem())
                outs = [fw.sb([128, D], F32) for _ in range(2)]; outsem = [fw.dsem() for _ in range(2)]
                uTv = uT.ap().rearrange("(k p) e -> p k e", p=128)
                evv = ev.ap().rearrange("(c p) d -> p c d", p=128)
                pcnt = 0; zc = 0; ec = 0
                for g0 in range(0, NTL, GT):
                    tiles = list(range(g0, min(g0 + GT, NTL)))
                    for ti, tl in enumerate(tiles):
                        fw.dma(h2[ti][:], H2T_d.v(H2T_d.ap()[tl].rearrange("p (k t) -> p k t", k=8), tl), lds[ti][0])
                        fw.dma(ss[ti][:].re("p h c k -> p (h c k)"), SS_d.v(SS_d.ap()[tl * 128:(tl + 1) * 128, :], tl), lds[ti][1])
                        fw.dma(stv[ti][:], ST_d.v(ST_d.ap()[tl * 128:(tl + 1) * 128, :], tl), lds[ti][2])
                        tt(fw, "pool", s1c[ti][:], ss[ti][:, :, 0, :], stv[ti][:, 8:16].re("p (h o) -> p h o", o=1).bc([128, 8, 128]), ALU.add)
                    def g_body(gslot, gsi, ti, pc):
                        nonlocal ec
                        z = gslot
                        for h in range(8):
                            Eb = Eb_[ec % NE]; gb = Gh_[ec % NE]; Sm = Sm_[ec % NE]; ec += 1
                            Ef = Eb[:].re("p a b -> p (a b)")
                            s2b = ss[ti][:, h, 1, :].re("p (o j) -> p o j", o=1).bc([128, 4, 128])
                            s1b = s1c[ti][:, h, pc * 4:pc * 4 + 4].re("p (i o) -> p i o", o=1).bc([128, 4, 128])
                            tt(fw, "pool", Sm[:], s2b, s1b, ALU.add)
                            yield
                            act(fw, Eb[:], Sm[:], AF.Exp)
                            yield
                            stt(fw, "dve", gb[:], Ef, stv[ti][:, h:h + 1], Ef, ALU.is_ge, ALU.mult)
                            yield
                            mm(fw, pG[z][:], ident_bf[:], gb[:], h == 0, h == 7)
                            yield
                        cp(fw, "act", GsR[gsi][:], pG[z][:])
                        yield

                    def t_body(tslot, gsi, ti, pc, ub, vb):
                        z = tslot
                        for k_ in range(8):
                            mm(fw, psc[z][:], h2[ti][:, k_, :], ub[:, k_, :], k_ == 0, k_ == 7)
                        yield
                        act(fw, gl[z][:], psc[z][:], AF.Gelu)
                        yield
                        tt(fw, "pool", Zt[z][:], gl[z][:], GsR[gsi][:], ALU.mult)
                        yield
                        for c_ in range(4):
                            tr(fw, ptz[z][:, c_ * 128:(c_ + 1) * 128], Zt[z][:, c_ * 128:(c_ + 1) * 128], ident_bf[:])
                        yield
                        cp(fw, "act", ZT[z][:].re("p c t -> p (c t)"), ptz[z][:])
                        yield
                        for nb in range(2):
                            for c_ in range(4):
                                mm(fw, pyy[z][:], ZT[z][:, c_, :], vb[:, c_, nb * 512:(nb + 1) * 512], c_ == 0, c_ == 3)
                            yield
                            ysl = ysb[ti][:, nb * 512:(nb + 1) * 512]
                            if pc == 0:
                                cp(fw, "dve", ysl, pyy[z][:])
                            else:
                                tt(fw, "dve", ysl, pyy[z][:], ysl, ALU.add)
                            yield

                    def it_list():
                        nonlocal pcnt
                        for pc in range(32):
                            e0 = pc * 512
                            ub = UTp[pcnt % 3]; vb = Vp[pcnt % 3]
                            fw.dma(ub[:], uT.v(uTv[:, :, e0:e0 + 512]), usem[pcnt % 3], queue="pool")
                            fw.dma(vb[:], ev.v(evv[:, pc * 4:pc * 4 + 4, :]), vsem[pcnt % 3], queue="pool")
                            pcnt += 1
                            for ti, tl in enumerate(tiles):
                                yield (ti, pc, ub, vb)
                    pending = it_list()
                    actG = {}; actT = {}; readyT = []
                    freeG = [0, 1]; freeT = [0, 1]
                    done = False
                    while True:
                        while freeG and not done and (len(actG) + len(readyT) + len(actT)) < NGS:
                            try:
                                a_ = next(pending)
                            except StopIteration:
                                done = True
                                break
                            s_ = freeG.pop(0)
                            gsi = itc % NGS; itc += 1
                            actG[s_] = (g_body(s_, gsi, a_[0], a_[1]), gsi, a_)
                        while freeT and readyT:
                            gsi, a_ = readyT.pop(0)
                            s_ = freeT.pop(0)
                            actT[s_] = t_body(s_, gsi, *a_)
                        if not actG and not actT and not readyT:
                            break
                        for s_ in list(actG.keys()):
                            try:
                                next(actG[s_][0])
                            except StopIteration:
                                _, gsi, a_ = actG.pop(s_)
                                freeG.append(s_)
                                readyT.append((gsi, a_))
                        for s_ in list(actT.keys()):
                            try:
                                next(actT[s_])
                            except StopIteration:
                                del actT[s_]
                                freeT.append(s_)
                    for ti, tl in enumerate(tiles):
                        fw.dma(x1l[:], X1_d.v(X1_d.ap()[tl * 128:(tl + 1) * 128, :], tl), x1ls)
                        tt(fw, "pool", tmpf[:], ysb[ti][:], g2_bc[:], ALU.mult)
                        tt(fw, "pool", tmpf[:], tmpf[:], x1l[:], ALU.add)
                        act(fw, junk[:], tmpf[:], AF.Square, accum=ssq[:])
                        act(fw, rstd[:], ssq[:], AF.Sqrt, bias=eps_col[:], scale=1.0 / D)
                        recip(fw, rstd[:], rstd[:])
                        ob_ = outs[tl % 2]
                        stt(fw, "dve", ob_[:], tmpf[:], rstd[:], fnw_s[:], ALU.mult, ALU.mult)
                        fw.dma(out.v(out.ap()[tl * 128:(tl + 1) * 128, :]), ob_[:], outsem[tl % 2])
                fw.phase_end()
            fw.stack = st

        fw.barrier()
        fw.replay(block)
        print("sems", getattr(fw, "n_sems", 0), "instrs", fw.n_instr, "waits", fw.n_wait, {k: len(v) for k, v in fw.lists.items()})
    return nc, inp, out, dbg_t, dict(QT_d=QT_d, KT_d=KT_d, Ktm_d=Ktm_d, Vtm_d=Vtm_d, GA_d=GA_d, GB_d=GB_d, OA_d=OA_d)


A_QKV = 3072; A_GATE = 1024; A_AB = 32
OFF_GA = 3072; OFF_AB = 4096; OFF_QB = 4128; OFF_F = 5152; OFF_IB = 7200; OFF_GB = 8224; OFF_MG = 9248


def prep_core(inp, core, W):
    b, hf = core // 2, core % 2
    hs = [4 * hf + i for i in range(4)]
    S = 128 * W
    x = inp["x"][b]; ctx = inp["ctx"][b]
    f = np.float32
    d = {}
    d["xa"] = np.ascontiguousarray(np.concatenate([ctx, x], 0))
    SH = S // 2
    d["xh"] = np.ascontiguousarray(x[hf * SH:(hf + 1) * SH])
    hs_ = np.zeros((128, 2), f); hs_[:, hf] = 1.0
    d["hsel"] = hs_
    cc = np.stack([inp["c"][b].reshape(8, 128).T, inp["c_ctx"].reshape(8, 128).T], -1)
    d["ccol"] = np.ascontiguousarray(cc.reshape(128, 16))
    d["w_ada"] = inp["w_ada"][0]
    d["b_adaT"] = np.ascontiguousarray(inp["b_ada"][0].reshape(48, 128).T)
    d["b_row"] = np.ascontiguousarray(inp["b_ada"][0].reshape(1, -1))
    d["n1"] = np.ascontiguousarray(inp["norm1_w"][0].reshape(8, 128).T)
    d["n2"] = np.ascontiguousarray(inp["norm2_w"][0].reshape(8, 128).T)
    d["fnw"] = np.ascontiguousarray(np.broadcast_to(inp["final_norm_w"], (128, 1024)))
    w_in = inp["w_in"][0]
    hc = lambda off: np.concatenate([np.arange(off + h * 128, off + (h + 1) * 128) for h in hs])
    colsA = np.concatenate([hc(0), hc(1024), hc(2048)])
    d["wA"] = np.ascontiguousarray(w_in[:, colsA])
    d["wGA"] = np.ascontiguousarray(w_in[:, hc(OFF_GA)])
    ab_cols = np.array([OFF_AB + seg * 8 + h for seg in range(4) for h in hs])
    d["wAB"] = np.ascontiguousarray(w_in[:, ab_cols])
    d["wBq"] = np.ascontiguousarray(w_in[:, hc(OFF_QB)])
    d["wBf"] = np.ascontiguousarray(w_in[:, np.concatenate([hc(OFF_F), hc(OFF_F + 1024)])])
    d["wBv"] = np.ascontiguousarray(w_in[:, hc(OFF_IB)])
    d["wBg"] = np.ascontiguousarray(w_in[:, hc(OFF_GB)])
    d["wMG"] = np.ascontiguousarray(w_in[:, OFF_MG:])
    cw = inp["conv_w"][0][:, colsA]
    d["convw"] = np.ascontiguousarray(cw.reshape(3, 12, 128).transpose(2, 1, 0).reshape(128, 36))
    gp = np.concatenate([inp["dt_bias"][0][0, hs], inp["dt_bias"][0][1, hs], inp["a_log"][0][0, hs], inp["a_log"][0][1, hs]])
    d["gpar"] = np.ascontiguousarray(np.broadcast_to(gp.astype(f), (128, 16)))
    d["gnw"] = np.ascontiguousarray(np.broadcast_to(inp["gdn_norm_w"][0], (128, 128)))
    d["hnw"] = np.ascontiguousarray(np.broadcast_to(inp["hg_norm_w"][0], (128, 128)))
    lb = inp["lb_logits"]
    d["lbl"] = np.ascontiguousarray(np.concatenate([lb[0, hc(0)].reshape(4, 128).T, lb[1, hc(0)].reshape(4, 128).T], 1))
    d["w_pa"] = inp["w_pa"][0]; d["w_pb"] = inp["w_pb"][0]; d["w_o"] = inp["w_o"][0]
    d["w_q"] = inp["w_query"][0]
    sk = inp["sub_keys"][0]
    d["skT"] = np.ascontiguousarray(sk.reshape(16, 128, 128).transpose(2, 0, 1).reshape(128, 16 * 128))
    d["uT"] = inp["_uT"]
    d["ev"] = inp["expert_v"][0]
    return {k: np.ascontiguousarray(v, dtype=f) for k, v in d.items()}


_W = 64
_CACHE = {}


def kernel(**inputs):
    inp = {k: np.asarray(v) for k, v in inputs.items()}
    inp["_uT"] = np.ascontiguousarray(inp["expert_u"][0].T)
    if "nc" not in _CACHE:
        _CACHE["nc"] = build(_W, ("all",), ())
    nc, inps, out, dbg_t, scr = _CACHE["nc"]
    in_maps = []
    for c in range(8):
        d = prep_core(inp, c, _W)
        in_maps.append({k: d[k] for k in inps})
    res = run_bass_kernel_spmd(nc, in_maps, core_ids=list(range(8)))
    S = 128 * _W
    SH = S // 2
    full = np.empty((4, S, 1024), np.float32)
    for c in range(8):
        b, hf = c // 2, c % 2
        full[b, hf * SH:(hf + 1) * SH] = np.asarray(res.results[c]["out"])
    return full
```
